# Optimizing a Trainium2 kernel written in Bass

```python
import math
import jax, jax.numpy as jnp
from jax import lax
import numpy as np

D_MODEL = 2048
BATCH = 8
SEQ = 2048
DEPTH = 2

GRID_W = 64
CTX_LEN = 256
HEAD_DIM = 128
N_MIX_HEADS = D_MODEL // HEAD_DIM
A_HEADS = N_MIX_HEADS // 2
A_KV_HEADS = 2
B_GROUPS = N_MIX_HEADS // 4
C_HEADS = N_MIX_HEADS - A_HEADS - B_GROUPS
A_WIDTH = A_HEADS * HEAD_DIM
A_KV_WIDTH = A_KV_HEADS * HEAD_DIM
HY_WIDTH = B_GROUPS * HEAD_DIM
C_WIDTH = C_HEADS * HEAD_DIM
D_MIX = A_WIDTH + HY_WIDTH + C_WIDTH
PROJ_SIZES = (A_WIDTH, A_KV_WIDTH, A_KV_WIDTH, 3 * HY_WIDTH, C_WIDTH, C_WIDTH, C_WIDTH)
D_IN = A_WIDTH + 2 * A_KV_WIDTH + 3 * HY_WIDTH + 3 * C_WIDTH
Q_BLOCK = 128
ROPE_THETA = 10000.0
HY_ORDER = 2
HY_SHORT = 3
HY_BANDS = 16
HY_POS_DIM = 1 + 2 * HY_BANDS
HY_FFN = 64
HY_DECAY_TARGET = 1e-2
HY_DECAY_SHORT_PCT = 0.3
HY_DECAY_LONG_PCT = 1.5
HY_DECAY_SHIFT = 0.05
WIN_R = 8
WIN_C = 16
D_FF = 4 * D_MODEL
DN_ALPHA = (2 * DEPTH) ** 0.25
DN_BETA = (8 * DEPTH) ** -0.25
EPS = 1e-6
NEG_INF = -1e30

kernel_name = 'hymba_gqa_hyena_natten_deepnorm_dit'


def _rms(x):
    xf = x.astype(jnp.float32)
    return xf * lax.rsqrt(jnp.mean(xf * xf, axis=-1, keepdims=True) + EPS)


def rms_norm(x, g):
    return (_rms(x) * g.astype(jnp.float32)).astype(x.dtype)


def layer_norm(x, g, b):
    xf = x.astype(jnp.float32)
    mu = jnp.mean(xf, axis=-1, keepdims=True)
    var = jnp.mean(jnp.square(xf - mu), axis=-1, keepdims=True)
    y = (xf - mu) * lax.rsqrt(var + EPS) * g.astype(jnp.float32) + b.astype(jnp.float32)
    return y.astype(x.dtype)


def to_heads(t, n_heads):
    return t.reshape(t.shape[:-1] + (n_heads, HEAD_DIM))


def split_projection(p):
    cuts, acc = [], 0
    for s in PROJ_SIZES[:-1]:
        acc += s
        cuts.append(acc)
    return jnp.split(p, cuts, axis=-1)


def axial_rope_tables(n_tokens):
    t = jnp.arange(n_tokens)
    row = (t // GRID_W).astype(jnp.float32)
    col = (t % GRID_W).astype(jnp.float32)
    n_pairs_axis = HEAD_DIM // 4
    inv_freq = ROPE_THETA ** (-jnp.arange(n_pairs_axis, dtype=jnp.float32) / n_pairs_axis)
    ang = jnp.concatenate([row[:, None] * inv_freq[None, :], col[:, None] * inv_freq[None, :]], axis=-1)
    return jnp.cos(ang), jnp.sin(ang)


def apply_rope(x, cos, sin):
    xf = x.astype(jnp.float32).reshape(x.shape[:-1] + (HEAD_DIM // 2, 2))
    x0, x1 = xf[..., 0], xf[..., 1]
    cs = cos[None, :, None, :]
    sn = sin[None, :, None, :]
    out = jnp.stack([x0 * cs - x1 * sn, x0 * sn + x1 * cs], axis=-1)
    return out.reshape(x.shape).astype(x.dtype)


def dense_attention(q, k, v):
    b, n, nh, hd = q.shape
    hkv = k.shape[2]
    qg = q.reshape(b, n, hkv, nh // hkv, hd)
    s = jnp.einsum('bqhgd,bkhd->bhgqk', qg, k).astype(jnp.float32) * (hd ** -0.5)
    p = jax.nn.softmax(s, axis=-1).astype(v.dtype)
    return jnp.einsum('bhgqk,bkhd->bqhgd', p, v).reshape(b, n, nh * hd)


def gqa_latent_attention(q, k, v, k_ctx, v_ctx):
    b, n = q.shape[:2]
    grp = A_HEADS // A_KV_HEADS
    k_all = jnp.concatenate([k, k_ctx], axis=1)
    v_all = jnp.concatenate([v, v_ctx], axis=1)
    qb = q.reshape(b, n // Q_BLOCK, Q_BLOCK, A_KV_HEADS, grp, HEAD_DIM).transpose(1, 0, 2, 3, 4, 5)
    scale = HEAD_DIM ** -0.5

    def one_block(q_blk):
        s = jnp.einsum('bqhgd,bkhd->bhgqk', q_blk, k_all).astype(jnp.float32) * scale
        p = jax.nn.softmax(s, axis=-1).astype(v_all.dtype)
        return jnp.einsum('bhgqk,bkhd->bqhgd', p, v_all)

    o = lax.map(one_block, qb)
    return o.transpose(1, 0, 2, 3, 4, 5).reshape(b, n, A_WIDTH)


def hyena_filter_spectrum(n, w1, b1, freq, w2, b2, w3):
    f32 = jnp.float32
    t = jnp.linspace(0.0, 1.0, n, dtype=f32)
    bands = jnp.arange(1, HY_BANDS + 1, dtype=f32)
    ang = 2.0 * math.pi * t[:, None] * bands[None, :]
    feat = jnp.concatenate([t[:, None], jnp.cos(ang), jnp.sin(ang)], axis=-1)
    fr = freq.astype(f32)
    hdn = jnp.sin(fr * (feat @ w1.astype(f32) + b1.astype(f32)))
    hdn = jnp.sin(fr * (hdn @ w2.astype(f32) + b2.astype(f32)))
    filt = (hdn @ w3.astype(f32)).reshape(n, HY_ORDER, 2, HY_WIDTH)
    max_decay = math.log(HY_DECAY_TARGET) / HY_DECAY_SHORT_PCT
    min_decay = math.log(HY_DECAY_TARGET) / HY_DECAY_LONG_PCT
    deltas = jnp.abs(jnp.linspace(min_decay, max_decay, HY_WIDTH, dtype=f32))
    window = jnp.exp(-t[:, None] * deltas[None, :]) + HY_DECAY_SHIFT
    filt = filt * window[:, None, None, :]
    fwd = filt[:, :, 0]
    bwd = filt[:, :, 1]
    circ = jnp.concatenate([fwd, jnp.zeros_like(fwd[:1]), bwd[1:][::-1]], axis=0)
    circ = circ / (jnp.sum(jnp.abs(circ), axis=0, keepdims=True) + EPS)
    return jnp.fft.rfft(circ, axis=0)


def hyena_operator(proj, short_w, short_b, spec, skip):
    n = proj.shape[1]
    half = HY_SHORT // 2
    pad = jnp.pad(proj, ((0, 0), (half, half), (0, 0)))
    u = short_b
    for j in range(HY_SHORT):
        u = u + short_w[j] * pad[:, j:j + n]
    v, g1, g2 = jnp.split(u, 3, axis=-1)
    z = v.astype(jnp.float32)
    for o, gate in enumerate((g1, g2)):
        zs = jnp.fft.rfft(z, n=2 * n, axis=1)
        y = jnp.fft.irfft(zs * spec[None, :, o], n=2 * n, axis=1)[:, :n]
        z = gate.astype(jnp.float32) * (y + skip[o].astype(jnp.float32) * z)
    return z.astype(proj.dtype)


def neighborhood_attention(q, k, v, k_ctx, v_ctx, rpb):
    b, n, nh, hd = q.shape
    rows = n // GRID_W
    kr = min(WIN_R, rows)
    r = jnp.arange(rows)
    col = jnp.arange(GRID_W)
    row_idx = jnp.clip(r - kr // 2, 0, rows - kr)[:, None] + jnp.arange(kr)[None, :]
    c_start = jnp.clip(col - WIN_C // 2, 0, GRID_W - WIN_C)
    col_mask = (col[None, :] >= c_start[:, None]) & (col[None, :] < c_start[:, None] + WIN_C)
    qg = q.reshape(b, rows, GRID_W, nh, hd)
    k_band = k.reshape(b, rows, GRID_W, nh, hd)[:, row_idx]
    v_band = v.reshape(b, rows, GRID_W, nh, hd)[:, row_idx]
    scale = hd ** -0.5
    s_loc = jnp.einsum('brchd,briwhd->bhrciw', qg, k_band).astype(jnp.float32) * scale
    d_row = (row_idx - r[:, None] + WIN_R - 1)[:, None, :, None]
    d_col = jnp.clip(col[None, :] - col[:, None] + WIN_C - 1, 0, 2 * WIN_C - 2)[None, :, None, :]
    bias = rpb.astype(jnp.float32)[:, d_row, d_col]
    s_loc = jnp.where(col_mask[None, None, None, :, None, :], s_loc + bias[None], NEG_INF)
    s_loc = s_loc.reshape(b, nh, rows, GRID_W, kr * GRID_W)
    s_ctx = jnp.einsum('brchd,bkhd->bhrck', qg, k_ctx).astype(jnp.float32) * scale
    p = jax.nn.softmax(jnp.concatenate([s_loc, s_ctx], axis=-1), axis=-1).astype(v.dtype)
    p_loc = p[..., :kr * GRID_W].reshape(b, nh, rows, GRID_W, kr, GRID_W)
    p_ctx = p[..., kr * GRID_W:]
    o = jnp.einsum('bhrciw,briwhd->brchd', p_loc, v_band) + jnp.einsum('bhrck,bkhd->brchd', p_ctx, v_ctx)
    return o.reshape(b, n, nh * hd)


def merge_groups(o_a, o_b, o_c, g):
    o = jnp.concatenate([_rms(o_a), _rms(o_b), _rms(o_c)], axis=-1)
    return (o * g.astype(jnp.float32)).astype(o_a.dtype)


def sq_relu_mlp(u, w1, w2):
    return jnp.square(jax.nn.relu(u @ w1)) @ w2


def setup_inputs(seed: int = 0) -> dict:
    key = jax.random.key(seed)
    ks = jax.random.split(key, 28)
    f32 = jnp.float32
    L = DEPTH
    D = D_MODEL

    def nrm(k, shape, s):
        return s * jax.random.normal(k, shape, f32)

    return {
        'x': nrm(ks[0], (BATCH, SEQ, D), 1.0),
        'c': nrm(ks[1], (BATCH, D), 1.0),
        'ctx': nrm(ks[2], (BATCH, CTX_LEN, D), 1.0),
        'c_ctx': nrm(ks[3], (D,), 1.0),
        'w_mod': nrm(ks[4], (L, D, 6 * D), D ** -0.5),
        'b_mod': nrm(ks[5], (L, 6 * D), 0.02),
        'w_in': nrm(ks[6], (L, D, D_IN), D ** -0.5),
        'q_norm_g': 1.0 + nrm(ks[7], (L, HEAD_DIM), 0.02),
        'k_norm_g': 1.0 + nrm(ks[8], (L, HEAD_DIM), 0.02),
        'hy_short_w': nrm(ks[9], (L, HY_SHORT, 3 * HY_WIDTH), HY_SHORT ** -0.5),
        'hy_short_b': nrm(ks[10], (L, 3 * HY_WIDTH), 0.02),
        'hf_w1': nrm(ks[11], (L, HY_POS_DIM, HY_FFN), HY_POS_DIM ** -0.5),
        'hf_b1': nrm(ks[12], (L, HY_FFN), 0.02),
        'hf_freq': 1.0 + nrm(ks[13], (L, HY_FFN), 0.1),
        'hf_w2': nrm(ks[14], (L, HY_FFN, HY_FFN), HY_FFN ** -0.5),
        'hf_b2': nrm(ks[15], (L, HY_FFN), 0.02),
        'hf_w3': nrm(ks[16], (L, HY_FFN, HY_ORDER * 2 * HY_WIDTH), HY_FFN ** -0.5),
        'hy_bias': nrm(ks[17], (L, HY_ORDER, HY_WIDTH), 1.0),
        'nat_rpb': nrm(ks[18], (L, C_HEADS, 2 * WIN_R - 1, 2 * WIN_C - 1), 0.02),
        'g_mix': 1.0 + nrm(ks[19], (L, D_MIX), 0.02),
        'w_out': nrm(ks[20], (L, D_MIX, D), (D_MIX ** -0.5) * DN_BETA),
        'ln1_g': 1.0 + nrm(ks[21], (L, D), 0.02),
        'ln1_b': nrm(ks[22], (L, D), 0.02),
        'w1': nrm(ks[23], (L, D, D_FF), D ** -0.5),
        'w2': nrm(ks[24], (L, D_FF, D), (D_FF ** -0.5) * DN_BETA),
        'ln2_g': 1.0 + nrm(ks[25], (L, D), 0.02),
        'ln2_b': nrm(ks[26], (L, D), 0.02),
    }


def reference(x, c, ctx, c_ctx, w_mod, b_mod, w_in, q_norm_g, k_norm_g, hy_short_w, hy_short_b,
              hf_w1, hf_b1, hf_freq, hf_w2, hf_b2, hf_w3, hy_bias, nat_rpb, g_mix, w_out,
              ln1_g, ln1_b, w1, w2, ln2_g, ln2_b):
    n_lat = x.shape[1]
    n_ctx = ctx.shape[1]
    cos, sin = axial_rope_tables(n_lat)
    silu_c = jax.nn.silu(c)
    silu_cc = jax.nn.silu(c_ctx)
    h = ctx
    for l in range(DEPTH):
        keep_ctx = l < DEPTH - 1
        mod = silu_c @ w_mod[l] + b_mod[l]
        mod_c = silu_cc @ w_mod[l] + b_mod[l]
        sh1, sc1, gt1, sh2, sc2, gt2 = jnp.split(mod[:, None, :], 6, axis=-1)
        csh1, csc1, cgt1, csh2, csc2, cgt2 = jnp.split(mod_c, 6, axis=-1)

        u = x * (1.0 + sc1) + sh1
        uc = h * (1.0 + csc1) + csh1
        aq, ak, av, hyp, nq, nk, nv = split_projection(u @ w_in[l])
        caq, cak, cav, chyp, cnq, cnk, cnv = split_projection(uc @ w_in[l])
        ak_c = rms_norm(to_heads(cak, A_KV_HEADS), k_norm_g[l])
        av_c = to_heads(cav, A_KV_HEADS)
        nk_c = to_heads(cnk, C_HEADS)
        nv_c = to_heads(cnv, C_HEADS)

        q_a = apply_rope(rms_norm(to_heads(aq, A_HEADS), q_norm_g[l]), cos, sin)
        k_a = apply_rope(rms_norm(to_heads(ak, A_KV_HEADS), k_norm_g[l]), cos, sin)
        o_a = gqa_latent_attention(q_a, k_a, to_heads(av, A_KV_HEADS), ak_c, av_c)
        spec = hyena_filter_spectrum(n_lat, hf_w1[l], hf_b1[l], hf_freq[l], hf_w2[l], hf_b2[l], hf_w3[l])
        o_b = hyena_operator(hyp, hy_short_w[l], hy_short_b[l], spec, hy_bias[l])
        o_c = neighborhood_attention(to_heads(nq, C_HEADS), to_heads(nk, C_HEADS), to_heads(nv, C_HEADS),
                                     nk_c, nv_c, nat_rpb[l])
        mix = merge_groups(o_a, o_b, o_c, g_mix[l]) @ w_out[l]
        x = layer_norm(DN_ALPHA * x + gt1 * mix, ln1_g[l], ln1_b[l])
        if keep_ctx:
            co_a = dense_attention(rms_norm(to_heads(caq, A_HEADS), q_norm_g[l]), ak_c, av_c)
            spec_c = hyena_filter_spectrum(n_ctx, hf_w1[l], hf_b1[l], hf_freq[l], hf_w2[l], hf_b2[l], hf_w3[l])
            co_b = hyena_operator(chyp, hy_short_w[l], hy_short_b[l], spec_c, hy_bias[l])
            co_c = dense_attention(to_heads(cnq, C_HEADS), nk_c, nv_c)
            cmix = merge_groups(co_a, co_b, co_c, g_mix[l]) @ w_out[l]
            h = layer_norm(DN_ALPHA * h + cgt1 * cmix, ln1_g[l], ln1_b[l])

        x = layer_norm(DN_ALPHA * x + gt2 * sq_relu_mlp(x * (1.0 + sc2) + sh2, w1[l], w2[l]), ln2_g[l], ln2_b[l])
        if keep_ctx:
            h = layer_norm(DN_ALPHA * h + cgt2 * sq_relu_mlp(h * (1.0 + csc2) + csh2, w1[l], w2[l]),
                           ln2_g[l], ln2_b[l])
    return x
```

```python
import math
import numpy as np
from contextlib import ExitStack
import concourse.bass as bass
import concourse.mybir as mybir
from concourse.alu_op_type import AluOpType as ALU
from concourse.bass_utils import run_bass_kernel_spmd

F32 = mybir.dt.float32
BF16 = mybir.dt.bfloat16
AF = mybir.ActivationFunctionType
AX = mybir.AxisListType

D = 2048
NLAT = 2048
NCTX = 256
T = NLAT + NCTX
NTB = T // 128
DIN = 4608
DFF = 8192
DEPTH = 2
HD = 128
EPS = 1e-6
DN_ALPHA = (2 * DEPTH) ** 0.25
GRID_W = 64
NEG = -30000.0
NF = 17
NFC = 3
TWO_PI = 2.0 * math.pi

C_AQ, C_AK, C_AV, C_HY, C_NQ, C_NK, C_NV = 0, 1024, 1280, 1536, 3072, 3584, 4096


class Sem:
    __slots__ = ("h", "count", "dma")

    def __init__(self, h, dma):
        self.h = h
        self.count = 0
        self.dma = dma


class Res:
    __slots__ = ("w", "r", "rp")

    def __init__(self):
        self.w = {}
        self.r = {}
        self.rp = {}


class Eng:
    def __init__(self, kb, name, e):
        self.kb = kb
        self.name = name
        self.e = e
        self.sem = kb.new_sem(name, False)
        self.waited = {}

    def wait(self, sem, val):
        if sem.dma:
            val = sem.count
        if val <= 0:
            return
        if sem is self.sem and self.name == "pe":
            return
        k = id(sem)
        if self.waited.get(k, 0) >= val:
            return
        self.waited[k] = val
        self.e.wait_ge(sem.h, val)


class KB:
    def __init__(self, nc):
        self.nc = nc
        self.nsem = 0
        self.E = {}
        for name, e in (("pe", nc.tensor), ("act", nc.scalar), ("dve", nc.vector),
                        ("pool", nc.gpsimd), ("sync", nc.sync)):
            self.E[name] = Eng(self, name, e)
        self.dsems = []
        self.dcur = 0
        self.psum = None
        self.banks = []
        self.bank_i = 0
        self.obank_i = 0

    def new_sem(self, name, dma):
        self.nsem += 1
        h = self.nc.alloc_semaphore(name=f"{name}_{self.nsem}")
        return Sem(h, dma)

    def dsem(self):
        if self.dcur == len(self.dsems):
            self.dsems.append(self.new_sem("dma", True))
        s = self.dsems[self.dcur]
        if s.count > 30000:
            s = self.new_sem("dma", True)
            self.dsems[self.dcur] = s
        self.dcur += 1
        return s

    def _waits(self, E, reads, writes, acc):
        for r in reads:
            for s, v in r.w.items():
                E.wait(s, v)
        for w in writes:
            if acc:
                for s, v in w.rp.items():
                    E.wait(s, v)
            else:
                for s, v in w.w.items():
                    E.wait(s, v)
                for s, v in w.r.items():
                    E.wait(s, v)

    def _mark(self, S, reads, writes, acc):
        for r in reads:
            r.r[S] = S.count
        for w in writes:
            if acc:
                w.w[S] = S.count
            else:
                w.rp = w.r
                w.w = {S: S.count}
                w.r = {}

    def op(self, eng, fn, reads=(), writes=(), mark=True, acc=False):
        E = self.E[eng]
        self._waits(E, reads, writes, acc)
        ins = fn(E.e)
        if mark:
            S = E.sem
            S.count += 1
            ins.then_inc(S.h, 1)
            self._mark(S, reads, writes, acc)
            if S.count > 30000:
                E.sem = self.new_sem(E.name, False)
        return ins

    def dma(self, q, out, in_, sem, reads=(), writes=(), acc=True, **kw):
        E = self.E[q]
        self._waits(E, reads, writes, acc)
        ins = E.e.dma_start(out=out, in_=in_, **kw)
        sem.count += 16
        ins.then_inc(sem.h, 16)
        self._mark(sem, reads, writes, acc)
        return ins

    def barrier(self):
        sems = [E.sem for E in self.E.values()] + list(self.dsems)
        for E in self.E.values():
            for s in sems:
                if s is E.sem:
                    continue
                E.wait(s, s.count)
        self.dcur = 0

    def bank(self, n=8):
        i = self.bank_i % n
        self.bank_i = (i + 1) % n
        return self.psum[:, i, :], self.banks[i]

    def obank(self):
        i = 6 + (self.obank_i % 2)
        self.obank_i += 1
        return self.psum[:, i, :], self.banks[i]


class DT:
    def __init__(self, nc, name, shape, dtype, kind="Internal"):
        self.t = nc.dram_tensor(name, list(shape), dtype, kind=kind)
        self.ap = self.t.ap()
        self.res = Res()
        self.shape = shape


class Tile:
    def __init__(self, kb, es, name, shape, dtype, dma=False):
        kb.ntile = getattr(kb, "ntile", 0) + 1
        self.t = es.enter_context(kb.nc.sbuf_tensor(f"{name}_{kb.ntile}", list(shape), dtype))
        self.res = Res()
        self.sem = kb.dsem() if dma else None

    def __getitem__(self, idx):
        return self.t[idx]


def tiles(kb, es, name, n, shape, dtype, dma=False):
    return [Tile(kb, es, f"{name}{i}", shape, dtype, dma) for i in range(n)]


class Prog:
    def __init__(self, dbg=False, n_layers=DEPTH, stop_after=None):
        self.dbg = dbg
        self.n_layers = n_layers
        self.stop_after = stop_after
        nc = bass.Bass("TRN2", target_bir_lowering=False)
        self.nc = nc
        self.kb = KB(nc)
        self.glob = ExitStack()
        self.inputs = {}
        self.scratch = {}
        self.outputs = {}

    def inp(self, name, shape, dtype=F32):
        d = DT(self.nc, name, shape, dtype, kind="ExternalInput")
        self.inputs[name] = d
        return d

    def scr(self, name, shape, dtype=F32):
        kind = "ExternalOutput" if (self.dbg and name in self.dbg) else "Internal"
        d = DT(self.nc, name, shape, dtype, kind=kind)
        self.scratch[name] = d
        return d

    def declare(self):
        L = DEPTH
        i = self.inp
        self.xin = i("xin", [T, D])
        self.cvec = i("cvec", [2, D])
        self.w_mod = i("w_mod", [L, D, 6 * D])
        self.b_mod = i("b_mod", [L, 6 * D])
        self.w_in = i("w_in", [L, D, DIN])
        self.q_norm_g = i("q_norm_g", [L, HD])
        self.k_norm_g = i("k_norm_g", [L, HD])
        self.hy_short_w = i("hy_short_w", [L, 3, 1536])
        self.hy_short_b = i("hy_short_b", [L, 1536])
        self.hf_w1 = i("hf_w1", [L, 33, 64])
        self.hf_b1 = i("hf_b1", [L, 64])
        self.hf_freq = i("hf_freq", [L, 64])
        self.hf_w2 = i("hf_w2", [L, 64, 64])
        self.hf_b2 = i("hf_b2", [L, 64])
        self.hf_w3 = i("hf_w3", [L, 64, 2048])
        self.hy_bias = i("hy_bias", [L, 2, 512])
        self.g_mix = i("g_mix", [L, D])
        self.w_out = i("w_out", [L, D, D])
        self.ln1_g = i("ln1_g", [L, D])
        self.ln1_b = i("ln1_b", [L, D])
        self.w1 = i("w1", [L, D, DFF])
        self.w2 = i("w2", [L, DFF, D])
        self.ln2_g = i("ln2_g", [L, D])
        self.ln2_b = i("ln2_b", [L, D])
        self.c_ident = i("c_ident", [128, 128])
        self.c_rope = i("c_rope", [2, NLAT, 64])
        self.c_nbias = i("c_nbias", [L, 4, 5, 128, 640])
        self.c_dft = i("c_dft", [2, NF, 128, NF, 128], BF16)
        self.c_dftc = i("c_dftc", [2, NFC, 128, NFC, 128], BF16)
        self.c_wf = i("c_wf", [2, 128, NF])
        self.c_feat = i("c_feat", [33, NLAT])
        self.c_featc = i("c_featc", [33, NCTX])
        self.c_win = i("c_win", [NLAT, 512])
        self.c_winc = i("c_winc", [NCTX, 512])
        self.yout = DT(self.nc, "yout", [NLAT, D], F32, kind="ExternalOutput")
        s = self.scr
        self.MODd = [s(f"MODd{l}", [2, 6 * D]) for l in range(L)]
        self.P = [s(f"P{l}", [T, DIN]) for l in range(L)]
        self.QT = [s(f"QT{l}", [10, 128, T], BF16) for l in range(L)]
        self.NQT = [s(f"NQT{l}", [8, 128, T], BF16) for l in range(L)]
        self.U = [s(f"U{l}", [T, 1536]) for l in range(L)]
        self.HS = [s(f"HS{l}", [NF * 128, 4, 512]) for l in range(L)]
        self.HSC = [s(f"HSC{l}", [NFC * 128, 4, 512]) for l in range(L)]
        self.Z1 = [s(f"Z1_{l}", [T, 512]) for l in range(L)]
        self.OMIX = [s(f"OMIX{l}", [T, D]) for l in range(L)]
        self.X1 = [s(f"X1_{l}", [T, D]) for l in range(L)]
        self.X2 = [s(f"X2_{l}", [T, D]) for l in range(L)]
        self.HT = [s(f"HT{l}", [DFF, T], BF16) for l in range(L)]

    def setup_globals(self):
        kb, nc, es = self.kb, self.nc, self.glob
        ps = es.enter_context(nc.psum_tensor("psum", [128, 8, 512], F32))
        kb.psum = ps
        kb.banks = [Res() for _ in range(8)]
        self.ident = Tile(kb, es, "ident", [128, 128], F32, dma=True)
        self.identb = Tile(kb, es, "identb", [128, 128], BF16)
        self.ones = Tile(kb, es, "ones", [128, 128], BF16)
        kb.dma("sync", self.ident[:], self.c_ident.ap, self.ident.sem, writes=[self.ident.res])
        kb.op("dve", lambda e: e.tensor_copy(out=self.identb[:], in_=self.ident[:]),
              reads=[self.ident.res], writes=[self.identb.res])
        kb.op("dve", lambda e: e.memset(self.ones[:], 1.0), writes=[self.ones.res])
        kb.barrier()
        kb.dcur = 1

    def load_fm(self, es, name, vec_ap, n, out_ap, out_res, acc=False, src_res=None):
        kb = self.kb
        stg = Tile(kb, es, name + "_stg", [n, 128], F32, dma=True)
        kb.dma("sync", stg[:], vec_ap.rearrange("(j p) -> j p", p=128), stg.sem,
               reads=([src_res] if src_res is not None else []), writes=[stg.res], acc=False)
        pb, pr = kb.bank()
        kb.op("pe", lambda e: e.transpose(out=pb[:, 0:n], in_=stg[:], identity=self.ident[0:n, 0:n]),
              reads=[stg.res, self.ident.res], writes=[pr])
        kb.op("dve", lambda e: e.tensor_copy(out=out_ap, in_=pb[:, 0:n]), reads=[pr], writes=[out_res], acc=acc)

    def split_bf16(self, src_ap, src_res, hi_ap, lo_ap, dst_res, acc=False):
        kb = self.kb
        kb.op("dve", lambda e: e.tensor_copy(out=hi_ap, in_=src_ap), reads=[src_res], writes=[dst_res], acc=acc)
        kb.op("dve", lambda e: e.tensor_tensor(out=lo_ap, in0=src_ap, in1=hi_ap, op=ALU.subtract),
              reads=[src_res, dst_res], writes=[dst_res], acc=True)

    def mm3(self, pb, pr, lhi, llo, rhi, rlo, reads):
        kb = self.kb
        kb.op("pe", lambda e: e.matmul(pb, lhsT=lhi, rhs=rhi, start=True, stop=False), reads=reads, writes=[pr], mark=False)
        kb.op("pe", lambda e: e.matmul(pb, lhsT=lhi, rhs=rlo, start=False, stop=False), reads=reads, writes=[pr], mark=False)
        kb.op("pe", lambda e: e.matmul(pb, lhsT=llo, rhs=rhi, start=False, stop=True), reads=reads, writes=[pr])

    def stage_end(self, es):
        self.kb.barrier()
        es.close()
        self.kb.dcur = 1

    def stage_mod(self):
        kb, nc = self.kb, self.nc
        es = ExitStack()
        cT = Tile(kb, es, "cT", [128, 16, 2], F32)
        sT = Tile(kb, es, "sT", [128, 16, 2], BF16)
        for r in range(2):
            self.load_fm(es, f"cfm{r}", self.cvec.ap[r], 16, cT[:, :, r], cT.res, acc=(r > 0))
        kb.op("act", lambda e: e.activation(out=sT[:], in_=cT[:], func=AF.Silu),
              reads=[cT.res], writes=[sT.res])
        wp = tiles(kb, es, "wmp", 3, [128, 16, 512], BF16, dma=True)
        bm = Tile(kb, es, "bm", [2, 6 * D], F32, dma=True)
        mo = Tile(kb, es, "mo", [2, 6 * D], F32, dma=True)
        it = 0
        for l in range(self.n_layers):
            kb.dma("sync", bm[:], self.b_mod.ap[l:l + 1, :].broadcast_to([2, 6 * D]), bm.sem,
                   writes=[bm.res], acc=False)
            for j in range(24):
                w = wp[it % 3]
                it += 1
                kb.dma("pool", w[:], self.w_mod.ap[l, :, j * 512:(j + 1) * 512].rearrange("(kc p) n -> p kc n", p=128),
                       w.sem, writes=[w.res], acc=False)
                pb, pr = kb.bank()
                for kc in range(16):
                    kb.op("pe", lambda e, kc=kc: e.matmul(pb[0:2, :], lhsT=sT[:, kc, :], rhs=w[:, kc, :],
                                                          start=(kc == 0), stop=(kc == 15)),
                          reads=[sT.res, w.res], writes=[pr], mark=(kc == 15))
                kb.op("dve", lambda e: e.tensor_tensor(out=mo[:, j * 512:(j + 1) * 512], in0=pb[0:2, :],
                                                       in1=bm[:, j * 512:(j + 1) * 512], op=ALU.add),
                      reads=[pr, bm.res], writes=[mo.res], acc=(j > 0))
            kb.dma("sync", self.MODd[l].ap, mo[:], mo.sem, reads=[mo.res], writes=[self.MODd[l].res])
        self.stage_end(es)

    def load_modf(self, es, l):
        kb = self.kb
        MF = Tile(kb, es, "MF", [128, 2, 96], F32)
        MF1 = Tile(kb, es, "MF1", [128, 2, 96], F32)
        for r in range(2):
            self.load_fm(es, f"mfm{r}", self.MODd[l].ap[r], 96, MF[:, r, :], MF.res, acc=(r > 0),
                         src_res=self.MODd[l].res)
        kb.op("dve", lambda e: e.tensor_scalar(out=MF1[:], in0=MF[:], scalar1=1.0, scalar2=None, op0=ALU.add),
              reads=[MF.res], writes=[MF1.res])
        return MF, MF1

    def build_ut_block(self, xt, UT, ut_res, col0, scale_fn, bias_fn, extra_reads):
        kb = self.kb
        for g in range(4):
            pb, pr = kb.bank()
            for i in range(4):
                kc = g * 4 + i
                kb.op("pe", lambda e, kc=kc, i=i: e.transpose(out=pb[:, i * 128:(i + 1) * 128],
                                                              in_=xt[:, kc * 128:(kc + 1) * 128],
                                                              identity=self.ident[:]),
                      reads=[xt.res, self.ident.res], writes=[pr], mark=(i == 3))
            for i in range(4):
                kc = g * 4 + i
                sc, bi = scale_fn(kc), bias_fn(kc)
                kb.op("act", lambda e, kc=kc, i=i, sc=sc, bi=bi: e.activation(
                    out=UT[:, kc, col0:col0 + 128], in_=pb[:, i * 128:(i + 1) * 128],
                    func=AF.Identity, scale=sc, bias=bi),
                    reads=[pr] + extra_reads, writes=[ut_res], acc=True)

    def stage_inproj(self, l, xsrc):
        kb = self.kb
        es = ExitStack()
        MF, MF1 = self.load_modf(es, l)
        UT = Tile(kb, es, "UT", [128, 16, T], BF16)
        xts = tiles(kb, es, "xt", 2, [128, D], F32, dma=True)
        utres = [Res() for _ in range(NTB)]
        for tb in range(NTB):
            xt = xts[tb % 2]
            kb.dma("sync", xt[:], xsrc.ap[tb * 128:(tb + 1) * 128, :], xt.sem, reads=[xsrc.res],
                   writes=[xt.res], acc=False)
            r = 0 if tb < 16 else 1
            self.build_ut_block(xt, UT, utres[tb], tb * 128,
                                lambda kc: MF1[:, r, 16 + kc:17 + kc], lambda kc: MF[:, r, kc:kc + 1],
                                [MF.res, MF1.res])
        wp = tiles(kb, es, "wip", 3, [128, 16, 512], BF16, dma=True)
        st = tiles(kb, es, "pst", 4, [128, 512], F32, dma=True)
        si = 0
        for j in range(DIN // 512):
            w = wp[j % 3]
            kb.dma("pool", w[:], self.w_in.ap[l, :, j * 512:(j + 1) * 512].rearrange("(kc p) n -> p kc n", p=128),
                   w.sem, writes=[w.res], acc=False)
            for tb in range(NTB):
                pb, pr = kb.bank()
                for kc in range(16):
                    kb.op("pe", lambda e, kc=kc: e.matmul(pb, lhsT=UT[:, kc, tb * 128:(tb + 1) * 128], rhs=w[:, kc, :],
                                                          start=(kc == 0), stop=(kc == 15)),
                          reads=[utres[tb], w.res], writes=[pr], mark=(kc == 15))
                s = st[si % 4]
                eng = "act" if si % 2 == 0 else "dve"
                si += 1
                if eng == "act":
                    kb.op("act", lambda e: e.copy(out=s[:], in_=pb), reads=[pr], writes=[s.res])
                else:
                    kb.op("dve", lambda e: e.tensor_copy(out=s[:], in_=pb), reads=[pr], writes=[s.res])
                kb.dma("sync", self.P[l].ap[tb * 128:(tb + 1) * 128, j * 512:(j + 1) * 512], s[:], s.sem,
                       reads=[s.res], writes=[self.P[l].res])
        self.stage_end(es)


    def stage_qkprep(self, l):
        kb = self.kb
        es = ExitStack()
        scale = HD ** -0.5
        G = Tile(kb, es, "G", [128, 2, 128], F32, dma=True)
        kb.dma("sync", G[:, 0, :], self.q_norm_g.ap[l:l + 1, :].broadcast_to([128, 128]), G.sem, writes=[G.res])
        kb.dma("sync", G[:, 1, :], self.k_norm_g.ap[l:l + 1, :].broadcast_to([128, 128]), G.sem, writes=[G.res])
        kb.op("dve", lambda e: e.tensor_scalar(out=G[:, 0, :], in0=G[:, 0, :], scalar1=scale, scalar2=None, op0=ALU.mult),
              reads=[G.res], writes=[G.res])
        t1s = tiles(kb, es, "t1", 2, [128, 1280], F32, dma=True)
        t2s = tiles(kb, es, "t2", 2, [128, 1024], F32, dma=True)
        css = tiles(kb, es, "cs", 2, [128, 2, 64], F32, dma=True)
        sq = Tile(kb, es, "sq", [128, 1280], F32)
        ss = Tile(kb, es, "ss", [128, 10], F32)
        sd = Tile(kb, es, "sd", [128, 10], F32)
        rstd = Tile(kb, es, "rstd", [128, 10], F32)
        xn = Tile(kb, es, "xn", [128, 10, 128], F32)
        ra = Tile(kb, es, "ra", [128, 10, 64], F32)
        rb = Tile(kb, es, "rb", [128, 10, 64], F32)
        rc = Tile(kb, es, "rc", [128, 10, 64], F32)
        rd = Tile(kb, es, "rd", [128, 10, 64], F32)
        xrs = tiles(kb, es, "xr", 2, [128, 18, 128], BF16)
        stgs = tiles(kb, es, "qstg", 2, [128, 18, 128], BF16, dma=True)
        epsb = Tile(kb, es, "epsb", [128, 1], F32)
        kb.op("dve", lambda e: e.memset(epsb[:], EPS), writes=[epsb.res])
        P = self.P[l]
        for tb in range(NTB):
            t1, t2, cs, xr, stg = t1s[tb % 2], t2s[tb % 2], css[tb % 2], xrs[tb % 2], stgs[tb % 2]
            rows = slice(tb * 128, (tb + 1) * 128)
            kb.dma("sync", t1[:], P.ap[rows, 0:1280], t1.sem, reads=[P.res], writes=[t1.res], acc=False)
            kb.dma("sync", t2[:], P.ap[rows, C_NQ:C_NQ + 1024], t2.sem, reads=[P.res], writes=[t2.res], acc=False)
            lat = tb < 16
            if lat:
                kb.dma("sync", cs[:], self.c_rope.ap[:, rows, :].rearrange("c t i -> t c i"), cs.sem,
                       writes=[cs.res], acc=False)
            kb.op("act", lambda e: e.activation(out=sq[:], in_=t1[:], func=AF.Square), reads=[t1.res], writes=[sq.res])
            kb.op("dve", lambda e: e.tensor_reduce(out=ss[:], in_=sq[:].rearrange("p (h d) -> p h d", d=128),
                                                   axis=AX.X, op=ALU.add), reads=[sq.res], writes=[ss.res])
            kb.op("act", lambda e: e.activation(out=sd[:], in_=ss[:], func=AF.Sqrt, scale=1.0 / HD, bias=epsb[:]),
                  reads=[ss.res, epsb.res], writes=[sd.res])
            kb.op("dve", lambda e: e.reciprocal(out=rstd[:], in_=sd[:]), reads=[sd.res], writes=[rstd.res])
            for h in range(10):
                gi = 0 if h < 8 else 1
                kb.op("dve", lambda e: e.scalar_tensor_tensor(out=xn[:, h, :], in0=t1[:, h * 128:(h + 1) * 128],
                                                              scalar=rstd[:, h:h + 1], in1=G[:, gi, :],
                                                              op0=ALU.mult, op1=ALU.mult),
                      reads=[t1.res, rstd.res, G.res], writes=[xn.res], acc=(h > 0))
            if lat:
                xv = xn[:].rearrange("p h (i two) -> p h i two", two=2)
                x0, x1 = xv[:, :, :, 0], xv[:, :, :, 1]
                ov = xr[:, 0:10, :].rearrange("p h (i two) -> p h i two", two=2)
                o0, o1 = ov[:, :, :, 0], ov[:, :, :, 1]
                cb = cs[:, 0, :].unsqueeze(1).broadcast_to([128, 10, 64])
                sb = cs[:, 1, :].unsqueeze(1).broadcast_to([128, 10, 64])
                kb.op("dve", lambda e: e.tensor_tensor(out=ra[:], in0=x0, in1=cb, op=ALU.mult),
                      reads=[xn.res, cs.res], writes=[ra.res])
                kb.op("pool", lambda e: e.tensor_tensor(out=rb[:], in0=x1, in1=sb, op=ALU.mult),
                      reads=[xn.res, cs.res], writes=[rb.res])
                kb.op("pool", lambda e: e.tensor_tensor(out=rc[:], in0=x0, in1=sb, op=ALU.mult),
                      reads=[xn.res, cs.res], writes=[rc.res])
                kb.op("dve", lambda e: e.tensor_tensor(out=rd[:], in0=x1, in1=cb, op=ALU.mult),
                      reads=[xn.res, cs.res], writes=[rd.res])
                kb.op("dve", lambda e: e.tensor_tensor(out=o0, in0=ra[:], in1=rb[:], op=ALU.subtract),
                      reads=[ra.res, rb.res], writes=[xr.res])
                kb.op("pool", lambda e: e.tensor_tensor(out=o1, in0=rc[:], in1=rd[:], op=ALU.add),
                      reads=[rc.res, rd.res], writes=[xr.res], acc=True)
            else:
                kb.op("act", lambda e: e.copy(out=xr[:, 0:10, :], in_=xn[:]), reads=[xn.res], writes=[xr.res])
            kb.op("act", lambda e: e.mul(out=xr[:, 10:14, :], in_=t2[:, 0:512].rearrange("p (h d) -> p h d", d=128), mul=scale),
                  reads=[t2.res], writes=[xr.res], acc=True)
            kb.op("act", lambda e: e.copy(out=xr[:, 14:18, :], in_=t2[:, 512:1024].rearrange("p (h d) -> p h d", d=128)),
                  reads=[t2.res], writes=[xr.res], acc=True)
            for g0 in range(0, 18, 8):
                n = min(8, 18 - g0)
                pb, pr = kb.bank()
                pb16 = pb.bitcast(BF16)
                for i in range(n):
                    kb.op("pe", lambda e: e.transpose(out=pb16[:, i * 128:(i + 1) * 128], in_=xr[:, g0 + i, :],
                                                      identity=self.identb[:]),
                          reads=[xr.res, self.identb.res], writes=[pr], mark=(i == n - 1))
                eng = "act" if (g0 // 8) % 2 == 0 else "dve"
                dst = stg[:, g0:g0 + n, :].rearrange("p h t -> p (h t)")
                if eng == "act":
                    kb.op("act", lambda e: e.copy(out=dst, in_=pb16[:, 0:n * 128]), reads=[pr], writes=[stg.res], acc=(g0 > 0))
                else:
                    kb.op("dve", lambda e: e.tensor_copy(out=dst, in_=pb16[:, 0:n * 128]), reads=[pr], writes=[stg.res], acc=True)
            kb.dma("sync", self.QT[l].ap[:, :, rows].rearrange("h d t -> d h t"), stg[:, 0:10, :], stg.sem,
                   reads=[stg.res], writes=[self.QT[l].res])
            kb.dma("sync", self.NQT[l].ap[:, :, rows].rearrange("h d t -> d h t"), stg[:, 10:18, :], stg.sem,
                   reads=[stg.res], writes=[self.NQT[l].res])
        self.stage_end(es)

    def attn_block(self, A, qT, qcol, kT, V, chunks, use_max, out_ap, out_res):
        kb = self.kb
        nch = len(chunks)
        it = A["it"]
        A["it"] += 1
        rs = A["rs"][it % 2]
        nm = A["nm"][it % 2]
        mx = A["mx"][it % 2]
        q = qT[:, qcol:qcol + 128]

        def s_mm(pb, pr, k0, ln):
            kb.op("pe", lambda e: e.matmul(pb[:, 0:ln], lhsT=q, rhs=kT[:, k0:k0 + ln], start=True, stop=True),
                  reads=[qT.res, kT.res], writes=[pr])

        if use_max:
            for ci, (k0, ln, bias) in enumerate(chunks):
                pb, pr = kb.bank(6)
                s_mm(pb, pr, k0, ln)
                if bias is not None:
                    sbt = A["sb"][A["sbi"] % 2]
                    A["sbi"] += 1
                    kb.op("dve", lambda e: e.tensor_tensor(out=sbt[:, 0:ln], in0=pb[:, 0:ln], in1=bias[0], op=ALU.add),
                          reads=[pr, bias[1]], writes=[sbt.res])
                    kb.op("dve", lambda e: e.reduce_max(out=mx[:, ci:ci + 1], in_=sbt[:, 0:ln], axis=AX.X),
                          reads=[sbt.res], writes=[mx.res], acc=(ci > 0))
                else:
                    kb.op("dve", lambda e: e.reduce_max(out=mx[:, ci:ci + 1], in_=pb[:, 0:ln], axis=AX.X),
                          reads=[pr], writes=[mx.res], acc=(ci > 0))
            kb.op("dve", lambda e: e.tensor_reduce(out=nm[:], in_=mx[:, 0:nch], axis=AX.X, op=ALU.max, negate=True),
                  reads=[mx.res], writes=[nm.res])
        ob, orr = kb.obank()
        nsub_tot = sum(ln // 128 for _, ln, _ in chunks)
        si = 0
        for ci, (k0, ln, bias) in enumerate(chunks):
            pb, pr = kb.bank(6)
            s_mm(pb, pr, k0, ln)
            src, src_res = pb[:, 0:ln], pr
            if bias is not None:
                sbt = A["sb"][A["sbi"] % 2]
                A["sbi"] += 1
                kb.op("dve", lambda e: e.tensor_tensor(out=sbt[:, 0:ln], in0=pb[:, 0:ln], in1=bias[0], op=ALU.add),
                      reads=[pr, bias[1]], writes=[sbt.res])
                src, src_res = sbt[:, 0:ln], sbt.res
            Pb = A["Pb"][A["pi"] % 3]
            PT = A["PT"][A["pi"] % 3]
            A["pi"] += 1
            if use_max:
                kb.op("act", lambda e: e.activation(out=Pb[:, 0:ln], in_=src, func=AF.Exp, bias=nm[:], accum_out=rs[:, ci:ci + 1]),
                      reads=[src_res, nm.res], writes=[Pb.res, rs.res], acc=False)
            else:
                kb.op("act", lambda e: e.activation(out=Pb[:, 0:ln], in_=src, func=AF.Exp, accum_out=rs[:, ci:ci + 1]),
                      reads=[src_res], writes=[Pb.res, rs.res], acc=False)
            nsub = ln // 128
            tb_, tr = kb.bank(6)
            tb16 = tb_.bitcast(BF16)
            for i in range(nsub):
                kb.op("pe", lambda e: e.transpose(out=tb16[:, i * 128:(i + 1) * 128], in_=Pb[:, i * 128:(i + 1) * 128],
                                                  identity=self.identb[:]),
                      reads=[Pb.res, self.identb.res], writes=[tr], mark=(i == nsub - 1))
            kb.op("dve", lambda e: e.tensor_copy(out=PT[:, 0:nsub, :].rearrange("p s t -> p (s t)"), in_=tb16[:, 0:nsub * 128]),
                  reads=[tr], writes=[PT.res])
            for i in range(nsub):
                kc = (k0 + i * 128) // 128
                last = (si == nsub_tot - 1)
                kb.op("pe", lambda e: e.matmul(ob[:, 0:128], lhsT=PT[:, i, :], rhs=V[:, kc, :], start=(si == 0), stop=last),
                      reads=[PT.res, V.res], writes=[orr], mark=(last or i == nsub - 1))
                si += 1
        rsum = A["rsum"][it % 2]
        kb.op("dve", lambda e: e.reduce_sum(out=rsum[:], in_=rs[:, 0:nch], axis=AX.X), reads=[rs.res], writes=[rsum.res])
        kb.op("dve", lambda e: e.reciprocal(out=rsum[:], in_=rsum[:]), reads=[rsum.res], writes=[rsum.res])
        ost = A["ost"][it % 4]
        kb.op("act", lambda e: e.activation(out=ost[:], in_=ob[:, 0:128], func=AF.Identity, scale=rsum[:]),
              reads=[orr, rsum.res], writes=[ost.res])
        kb.dma("sync", out_ap, ost[:], ost.sem, reads=[ost.res], writes=[out_res])

    def attn_ctx(self, es):
        kb = self.kb
        A = {"it": 0, "sbi": 0, "pi": 0}
        A["rs"] = tiles(kb, es, "a_rs", 2, [128, 8], F32)
        A["nm"] = tiles(kb, es, "a_nm", 2, [128, 1], F32)
        A["mx"] = tiles(kb, es, "a_mx", 2, [128, 8], F32)
        A["rsum"] = tiles(kb, es, "a_rsum", 2, [128, 1], F32)
        A["sb"] = tiles(kb, es, "a_sb", 2, [128, 512], F32)
        A["Pb"] = tiles(kb, es, "a_Pb", 3, [128, 512], BF16)
        A["PT"] = tiles(kb, es, "a_PT", 3, [128, 4, 128], BF16)
        A["ost"] = tiles(kb, es, "a_ost", 4, [128, 128], F32, dma=True)
        return A

    def stage_gqa(self, l):
        kb = self.kb
        es = ExitStack()
        A = self.attn_ctx(es)
        keep_ctx = l < DEPTH - 1
        kTs = tiles(kb, es, "kT", 2, [128, T], BF16, dma=True)
        Vs = tiles(kb, es, "V", 2, [128, NTB, 128], BF16, dma=True)
        qTs = tiles(kb, es, "qT", 2, [128, T], BF16, dma=True)
        P, QT, OM = self.P[l], self.QT[l], self.OMIX[l]
        lat_chunks = [(0, 512, None), (512, 512, None), (1024, 512, None), (1536, 512, None), (2048, 256, None)]
        for g in range(2):
            kT, V = kTs[g], Vs[g]
            kb.dma("sync", kT[:], QT.ap[8 + g], kT.sem, reads=[QT.res], writes=[kT.res], acc=False)
            kb.dma("pool", V[:], P.ap[:, C_AV + g * 128:C_AV + (g + 1) * 128].rearrange("(kc p) d -> p kc d", p=128),
                   V.sem, reads=[P.res], writes=[V.res], acc=False)
            for hh in range(4):
                h = g * 4 + hh
                qT = qTs[h % 2]
                kb.dma("sync", qT[:], QT.ap[h], qT.sem, reads=[QT.res], writes=[qT.res], acc=False)
                for qb in range(16):
                    self.attn_block(A, qT, qb * 128, kT, V, lat_chunks, False,
                                    OM.ap[qb * 128:(qb + 1) * 128, h * 128:(h + 1) * 128], OM.res)
                if keep_ctx:
                    for qb in (16, 17):
                        self.attn_block(A, qT, qb * 128, kT, V, [(2048, 256, None)], False,
                                        OM.ap[qb * 128:(qb + 1) * 128, h * 128:(h + 1) * 128], OM.res)
        self.stage_end(es)

    def stage_nat(self, l):
        kb = self.kb
        es = ExitStack()
        A = self.attn_ctx(es)
        keep_ctx = l < DEPTH - 1
        kTs = tiles(kb, es, "nkT", 2, [128, T], BF16, dma=True)
        Vs = tiles(kb, es, "nV", 2, [128, NTB, 128], BF16, dma=True)
        qTs = tiles(kb, es, "nqT", 2, [128, T], BF16, dma=True)
        nbs = tiles(kb, es, "nb", 2, [128, 5, 640], F32, dma=True)
        P, NQT, OM = self.P[l], self.NQT[l], self.OMIX[l]
        for h in range(4):
            kT, V, qT, nb = kTs[h % 2], Vs[h % 2], qTs[h % 2], nbs[h % 2]
            kb.dma("sync", kT[:], NQT.ap[4 + h], kT.sem, reads=[NQT.res], writes=[kT.res], acc=False)
            kb.dma("sync", qT[:], NQT.ap[h], qT.sem, reads=[NQT.res], writes=[qT.res], acc=False)
            kb.dma("sync", nb[:], self.c_nbias.ap[l, h].rearrange("s q k -> q s k"), nb.sem, writes=[nb.res], acc=False)
            kb.dma("pool", V[:], P.ap[:, C_NV + h * 128:C_NV + (h + 1) * 128].rearrange("(kc p) d -> p kc d", p=128),
                   V.sem, reads=[P.res], writes=[V.res], acc=False)
            col = 1536 + h * 128
            for qb in range(16):
                wb0 = min(max(qb - 2, 0), 11)
                pat = {0: 0, 1: 1, 14: 3, 15: 4}.get(qb, 2)
                chunks = [(wb0 * 128, 512, (nb[:, pat, 0:512], nb.res)),
                          ((wb0 + 4) * 128, 128, (nb[:, pat, 512:640], nb.res)),
                          (2048, 256, None)]
                self.attn_block(A, qT, qb * 128, kT, V, chunks, True,
                                OM.ap[qb * 128:(qb + 1) * 128, col:col + 128], OM.res)
            if keep_ctx:
                for qb in (16, 17):
                    self.attn_block(A, qT, qb * 128, kT, V, [(2048, 256, None)], True,
                                    OM.ap[qb * 128:(qb + 1) * 128, col:col + 128], OM.res)
        self.stage_end(es)

    def stage_hyshort(self, l):
        kb = self.kb
        es = ExitStack()
        P, U = self.P[l], self.U[l]
        wbc = Tile(kb, es, "wbc", [128, 4, 1536], F32, dma=True)
        for j in range(3):
            kb.dma("sync", wbc[:, j, :], self.hy_short_w.ap[l, j:j + 1, :].broadcast_to([128, 1536]), wbc.sem, writes=[wbc.res])
        kb.dma("sync", wbc[:, 3, :], self.hy_short_b.ap[l:l + 1, :].broadcast_to([128, 1536]), wbc.sem, writes=[wbc.res])
        hms = tiles(kb, es, "hm", 2, [128, 1536], F32, dma=True)
        h0s = tiles(kb, es, "h0", 2, [128, 1536], F32, dma=True)
        hps = tiles(kb, es, "hp", 2, [128, 1536], F32, dma=True)
        us = tiles(kb, es, "u", 2, [128, 1536], F32, dma=True)
        ta = Tile(kb, es, "hta", [128, 1536], F32)
        tb_ = Tile(kb, es, "htb", [128, 1536], F32)
        tc_ = Tile(kb, es, "htc", [128, 1536], F32)
        cs = slice(C_HY, C_HY + 1536)
        for tb in range(NTB):
            hm, h0, hp, u = hms[tb % 2], h0s[tb % 2], hps[tb % 2], us[tb % 2]
            r0 = tb * 128
            first = tb in (0, 16)
            last = tb in (15, 17)
            kb.dma("sync", h0[:], P.ap[r0:r0 + 128, cs], h0.sem, reads=[P.res], writes=[h0.res], acc=False)
            if first:
                kb.op("pool", lambda e: e.memset(hm[:], 0.0), writes=[hm.res])
                kb.dma("sync", hm[1:128, :], P.ap[r0:r0 + 127, cs], hm.sem, reads=[P.res, hm.res], writes=[hm.res], acc=True)
            else:
                kb.dma("sync", hm[:], P.ap[r0 - 1:r0 + 127, cs], hm.sem, reads=[P.res], writes=[hm.res], acc=False)
            if last:
                kb.op("pool", lambda e: e.memset(hp[:], 0.0), writes=[hp.res])
                kb.dma("sync", hp[0:127, :], P.ap[r0 + 1:r0 + 128, cs], hp.sem, reads=[P.res, hp.res], writes=[hp.res], acc=True)
            else:
                kb.dma("sync", hp[:], P.ap[r0 + 1:r0 + 129, cs], hp.sem, reads=[P.res], writes=[hp.res], acc=False)
            kb.op("dve", lambda e: e.tensor_tensor(out=ta[:], in0=hm[:], in1=wbc[:, 0, :], op=ALU.mult),
                  reads=[hm.res, wbc.res], writes=[ta.res])
            kb.op("pool", lambda e: e.tensor_tensor(out=tb_[:], in0=h0[:], in1=wbc[:, 1, :], op=ALU.mult),
                  reads=[h0.res, wbc.res], writes=[tb_.res])
            kb.op("pool", lambda e: e.tensor_tensor(out=tc_[:], in0=hp[:], in1=wbc[:, 2, :], op=ALU.mult),
                  reads=[hp.res, wbc.res], writes=[tc_.res])
            kb.op("dve", lambda e: e.tensor_tensor(out=ta[:], in0=ta[:], in1=tb_[:], op=ALU.add),
                  reads=[ta.res, tb_.res], writes=[ta.res])
            kb.op("pool", lambda e: e.tensor_tensor(out=tc_[:], in0=tc_[:], in1=wbc[:, 3, :], op=ALU.add),
                  reads=[tc_.res, wbc.res], writes=[tc_.res])
            kb.op("dve", lambda e: e.tensor_tensor(out=u[:], in0=ta[:], in1=tc_[:], op=ALU.add),
                  reads=[ta.res, tc_.res], writes=[u.res])
            kb.dma("sync", U.ap[r0:r0 + 128, :], u[:], u.sem, reads=[u.res], writes=[U.res])
        self.stage_end(es)

    def stage_hyfilter(self, l, ctx):
        kb = self.kb
        es = ExitStack()
        if ctx:
            n, NT, nf, dft, feat, win, HS, wfi = NCTX, 2, NFC, self.c_dftc, self.c_featc, self.c_winc, self.HSC[l], 1
        else:
            n, NT, nf, dft, feat, win, HS, wfi = NLAT, 16, NF, self.c_dft, self.c_feat, self.c_win, self.HS[l], 0
        w1t = Tile(kb, es, "w1t", [128, 128], F32, dma=True)
        w2t = Tile(kb, es, "w2t", [128, 128], F32, dma=True)
        kb.op("dve", lambda e: e.memset(w1t[:], 0.0), writes=[w1t.res])
        kb.op("dve", lambda e: e.memset(w2t[:], 0.0), writes=[w2t.res])
        w3t = Tile(kb, es, "w3t", [128, 2048], F32, dma=True)
        kb.op("dve", lambda e: e.memset(w3t[:], 0.0), writes=[w3t.res])
        cols = Tile(kb, es, "hcols", [128, 3], F32)
        crow = Tile(kb, es, "hcrow", [3, 128], F32, dma=True)
        kb.op("dve", lambda e: e.memset(crow[:], 0.0), writes=[crow.res])
        kb.dma("sync", w1t[0:33, 0:64], self.hf_w1.ap[l], w1t.sem, reads=[w1t.res], writes=[w1t.res])
        kb.dma("sync", w2t[0:64, 0:64], self.hf_w2.ap[l], w2t.sem, reads=[w2t.res], writes=[w2t.res])
        kb.dma("sync", w3t[0:64, :], self.hf_w3.ap[l], w3t.sem, reads=[w3t.res], writes=[w3t.res])
        for i, v in enumerate((self.hf_b1, self.hf_b2, self.hf_freq)):
            kb.dma("sync", crow[i:i + 1, 0:64], v.ap[l:l + 1, :], crow.sem, reads=[crow.res], writes=[crow.res])
        pbc, prc = kb.bank()
        kb.op("pe", lambda e: e.transpose(out=pbc[:, 0:3], in_=crow[:], identity=self.ident[0:3, 0:3]),
              reads=[crow.res, self.ident.res], writes=[prc])
        kb.op("dve", lambda e: e.tensor_copy(out=cols[:], in_=pbc[:, 0:3]), reads=[prc], writes=[cols.res])
        featT = Tile(kb, es, "featT", [128, n], F32, dma=True)
        kb.op("dve", lambda e: e.memset(featT[:], 0.0), writes=[featT.res])
        kb.dma("sync", featT[0:33, :], feat.ap, featT.sem, reads=[featT.res], writes=[featT.res])
        wf = Tile(kb, es, "wf", [128, NF], F32, dma=True)
        kb.dma("sync", wf[:], self.c_wf.ap[wfi], wf.sem, writes=[wf.res])
        wsp = Tile(kb, es, "wsp", [128, 2, 2, 128], BF16)
        w3s = Tile(kb, es, "w3s", [128, 2, 2048], BF16)
        fts = Tile(kb, es, "fts", [128, 2, n], BF16)
        h1s = Tile(kb, es, "h1s", [128, 2, n], BF16)
        h2s = Tile(kb, es, "h2s", [128, 2, n], BF16)
        self.split_bf16(w1t[:], w1t.res, wsp[:, 0, 0, :], wsp[:, 0, 1, :], wsp.res)
        self.split_bf16(w2t[:], w2t.res, wsp[:, 1, 0, :], wsp[:, 1, 1, :], wsp.res, acc=True)
        self.split_bf16(w3t[:], w3t.res, w3s[:, 0, :], w3s[:, 1, :], w3s.res)
        self.split_bf16(featT[:], featT.res, fts[:, 0, :], fts[:, 1, :], fts.res)
        hf = Tile(kb, es, "hf32", [128, 512], F32)
        arg = Tile(kb, es, "harg", [128, 512], F32)
        m1 = Tile(kb, es, "hm1", [128, 512], F32)
        PI = math.pi
        for (wi_, src, bcol, dst) in ((0, fts, 0, h1s), (1, h1s, 1, h2s)):
            for c0 in range(0, n, 512):
                cl = min(512, n - c0)
                pb, pr = kb.bank()
                self.mm3(pb[:, 0:cl], pr, wsp[:, wi_, 0, :], wsp[:, wi_, 1, :], src[:, 0, c0:c0 + cl], src[:, 1, c0:c0 + cl],
                         [wsp.res, src.res])
                a = arg[:, 0:cl]
                m = m1[:, 0:cl]
                kb.op("dve", lambda e: e.tensor_scalar(out=a, in0=pb[:, 0:cl], scalar1=cols[:, bcol:bcol + 1],
                                                       scalar2=cols[:, 2:3], op0=ALU.add, op1=ALU.mult),
                      reads=[pr, cols.res], writes=[arg.res])
                kb.op("dve", lambda e: e.tensor_scalar(out=m, in0=a, scalar1=PI, scalar2=-TWO_PI, op0=ALU.is_gt, op1=ALU.mult),
                      reads=[arg.res], writes=[m1.res])
                kb.op("dve", lambda e: e.tensor_tensor(out=a, in0=a, in1=m, op=ALU.add), reads=[arg.res, m1.res], writes=[arg.res])
                kb.op("dve", lambda e: e.tensor_scalar(out=m, in0=a, scalar1=-PI, scalar2=TWO_PI, op0=ALU.is_lt, op1=ALU.mult),
                      reads=[arg.res], writes=[m1.res])
                kb.op("dve", lambda e: e.tensor_tensor(out=a, in0=a, in1=m, op=ALU.add), reads=[arg.res, m1.res], writes=[arg.res])
                kb.op("dve", lambda e: e.tensor_scalar(out=a, in0=a, scalar1=PI, scalar2=-PI, op0=ALU.min, op1=ALU.max),
                      reads=[arg.res], writes=[arg.res])
                kb.op("act", lambda e: e.activation(out=hf[:, 0:cl], in_=a, func=AF.Sin), reads=[arg.res], writes=[hf.res])
                self.split_bf16(hf[:, 0:cl], hf.res, dst[:, 0, c0:c0 + cl], dst[:, 1, c0:c0 + cl], dst.res, acc=(c0 > 0))
        acc = Tile(kb, es, "hacc", [128, 2048], F32)
        fab = Tile(kb, es, "hfab", [128, 2048], F32)
        fws = tiles(kb, es, "hfw", 1, [128, 2048], F32)
        wts = tiles(kb, es, "hwin", 2, [128, 512], F32, dma=True)
        rn = Tile(kb, es, "hrn", [128, 2, 512], F32)
        tA = Tile(kb, es, "htA", [128, 2, 512], F32)
        tB = Tile(kb, es, "htB", [128, 2, 512], F32)
        A = Tile(kb, es, "hA", [128, NT, 2, 512], BF16)
        Bm = Tile(kb, es, "hBm", [128, NT, 2, 512], BF16)
        kb.op("dve", lambda e: e.memset(acc[:], 0.0), writes=[acc.res])
        it = 0
        for ps_ in range(2):
            for tc in range(NT):
                fw, wt_ = fws[0], wts[it % 2]
                it += 1
                kb.dma("sync", wt_[:], win.ap[tc * 128:(tc + 1) * 128, :], wt_.sem, writes=[wt_.res], acc=False)
                for q in range(4):
                    pb, pr = kb.bank()
                    self.mm3(pb, pr, h2s[:, 0, tc * 128:(tc + 1) * 128], h2s[:, 1, tc * 128:(tc + 1) * 128],
                             w3s[:, 0, q * 512:(q + 1) * 512], w3s[:, 1, q * 512:(q + 1) * 512], [h2s.res, w3s.res])
                    kb.op("dve", lambda e: e.tensor_tensor(out=fw[:, q * 512:(q + 1) * 512], in0=pb, in1=wt_[:], op=ALU.mult),
                          reads=[pr, wt_.res], writes=[fw.res], acc=(q > 0))
                fwv = fw[:].rearrange("p (o d c) -> p o d c", o=2, d=2)
                if tc == 0:
                    kb.op("dve", lambda e: e.memset(fwv[0:1, :, 1, :], 0.0), reads=[fw.res], writes=[fw.res])
                if ps_ == 0:
                    kb.op("dve", lambda e: e.scalar_tensor_tensor(out=fab[:], in0=fw[:], scalar=-1.0, in1=fw[:],
                                                                  op0=ALU.mult, op1=ALU.max),
                          reads=[fw.res], writes=[fab.res])
                    kb.op("dve", lambda e: e.tensor_tensor(out=acc[:], in0=acc[:], in1=fab[:], op=ALU.add),
                          reads=[fab.res, acc.res], writes=[acc.res])
                else:
                    kb.op("dve", lambda e: e.tensor_tensor(out=tA[:], in0=fwv[:, :, 0, :], in1=fwv[:, :, 1, :], op=ALU.add),
                          reads=[fw.res], writes=[tA.res])
                    kb.op("pool", lambda e: e.tensor_tensor(out=tB[:], in0=fwv[:, :, 0, :], in1=fwv[:, :, 1, :], op=ALU.subtract),
                          reads=[fw.res], writes=[tB.res])
                    kb.op("dve", lambda e: e.tensor_tensor(out=A[:, tc, :, :], in0=tA[:], in1=rn[:], op=ALU.mult),
                          reads=[tA.res, rn.res], writes=[A.res], acc=(tc > 0))
                    kb.op("pool", lambda e: e.tensor_tensor(out=Bm[:, tc, :, :], in0=tB[:], in1=rn[:], op=ALU.mult),
                          reads=[tB.res, rn.res], writes=[Bm.res], acc=(tc > 0))
            if ps_ == 0:
                bks = []
                self.split_bf16(acc[:], acc.res, w3s[:, 0, :], w3s[:, 1, :], w3s.res)
                for q in range(4):
                    pb, pr = kb.bank()
                    kb.op("pe", lambda e: e.matmul(pb, lhsT=self.ones[:], rhs=w3s[:, 0, q * 512:(q + 1) * 512], start=True, stop=False),
                          reads=[self.ones.res, w3s.res], writes=[pr], mark=False)
                    kb.op("pe", lambda e: e.matmul(pb, lhsT=self.ones[:], rhs=w3s[:, 1, q * 512:(q + 1) * 512], start=False, stop=True),
                          reads=[self.ones.res, w3s.res], writes=[pr])
                    bks.append((pb, pr))
                self.split_bf16(w3t[:], w3t.res, w3s[:, 0, :], w3s[:, 1, :], w3s.res)
                for o in range(2):
                    kb.op("act", lambda e: e.copy(out=tA[:, o, :], in_=bks[2 * o + 1][0]), reads=[bks[2 * o + 1][1]],
                          writes=[tA.res], acc=(o > 0))
                    kb.op("dve", lambda e: e.scalar_tensor_tensor(out=tB[:, o, :], in0=bks[2 * o][0], scalar=EPS, in1=tA[:, o, :],
                                                                  op0=ALU.add, op1=ALU.add),
                          reads=[bks[2 * o][1], tA.res], writes=[tB.res], acc=(o > 0))
                kb.op("dve", lambda e: e.reciprocal(out=rn[:], in_=tB[:]), reads=[tB.res], writes=[rn.res])
        Cps = tiles(kb, es, "hCp", 2, [128, nf, 128], BF16, dma=True)
        Sps = tiles(kb, es, "hSp", 2, [128, nf, 128], BF16, dma=True)
        hsts = tiles(kb, es, "hst", 1, [128, 4, 512], F32, dma=True)
        for j in range(nf):
            Cp, Sp, hst = Cps[j % 2], Sps[j % 2], hsts[0]
            kb.dma("sync", Cp[:], dft.ap[0, j], Cp.sem, writes=[Cp.res], acc=False)
            kb.dma("sync", Sp[:], dft.ap[1, j], Sp.sem, writes=[Sp.res], acc=False)
            for o in range(2):
                for ri, (pan, src) in enumerate(((Cp, A), (Sp, Bm))):
                    pb, pr = kb.bank()
                    for tc in range(NT):
                        kb.op("pe", lambda e: e.matmul(pb, lhsT=pan[:, tc, :], rhs=src[:, tc, o, :], start=(tc == 0), stop=(tc == NT - 1)),
                              reads=[pan.res, src.res], writes=[pr], mark=(tc == NT - 1))
                    first = (o == 0 and ri == 0)
                    kb.op("act", lambda e: e.activation(out=hst[:, o * 2 + ri, :], in_=pb, func=AF.Identity, scale=wf[:, j:j + 1]),
                          reads=[pr, wf.res], writes=[hst.res], acc=(not first))
            kb.dma("sync", HS.ap[j * 128:(j + 1) * 128], hst[:], hst.sem, reads=[hst.res], writes=[HS.res])
        self.stage_end(es)

    def stage_hyconv(self, l, ctx):
        kb = self.kb
        es = ExitStack()
        if ctx:
            n, NT, nf, dft, HS, r0 = NCTX, 2, NFC, self.c_dftc, self.HSC[l], NLAT
        else:
            n, NT, nf, dft, HS, r0 = NLAT, 16, NF, self.c_dft, self.HS[l], 0
        U, Z1, OM = self.U[l], self.Z1[l], self.OMIX[l]
        zbs = tiles(kb, es, "zb", 2, [128, NT, 512], BF16, dma=True)
        Y = Tile(kb, es, "hY", [128, nf, 2, 512], BF16)
        Cps = tiles(kb, es, "cCp", 2, [128, nf, 128], BF16, dma=True)
        Sps = tiles(kb, es, "cSp", 2, [128, nf, 128], BF16, dma=True)
        Hts = tiles(kb, es, "cHt", 2, [128, 2, 512], F32, dma=True)
        t1 = Tile(kb, es, "ct1", [128, 512], F32)
        t2 = Tile(kb, es, "ct2", [128, 512], F32)
        t3 = Tile(kb, es, "ct3", [128, 512], F32)
        t4 = Tile(kb, es, "ct4", [128, 512], F32)
        zts = tiles(kb, es, "czt", 2, [128, 512], F32, dma=True)
        gts = tiles(kb, es, "cgt", 2, [128, 512], F32, dma=True)
        osts = tiles(kb, es, "cost", 2, [128, 512], F32, dma=True)
        skip = Tile(kb, es, "cskip", [128, 2, 512], F32, dma=True)
        for o in range(2):
            kb.dma("sync", skip[:, o, :], self.hy_bias.ap[l, o:o + 1, :].broadcast_to([128, 512]), skip.sem, writes=[skip.res])
        pi = 0
        for o in range(2):
            zsrc_ap = U.ap[r0:r0 + n, 0:512] if o == 0 else Z1.ap[r0:r0 + n, :]
            zsrc_res = U.res if o == 0 else Z1.res
            zb = zbs[o]
            kb.dma("pool", zb[:], zsrc_ap.rearrange("(tc p) c -> p tc c", p=128), zb.sem, reads=[zsrc_res], writes=[zb.res], acc=False)
            for j in range(nf):
                Cp, Sp, Ht = Cps[pi % 2], Sps[pi % 2], Hts[pi % 2]
                pi += 1
                kb.dma("sync", Cp[:], dft.ap[0, j], Cp.sem, writes=[Cp.res], acc=False)
                kb.dma("sync", Sp[:], dft.ap[1, j], Sp.sem, writes=[Sp.res], acc=False)
                kb.dma("sync", Ht[:], HS.ap[j * 128:(j + 1) * 128, 2 * o:2 * o + 2, :], Ht.sem, reads=[HS.res], writes=[Ht.res], acc=False)
                pre, prr = kb.bank()
                for tc in range(NT):
                    kb.op("pe", lambda e: e.matmul(pre, lhsT=Cp[:, tc, :], rhs=zb[:, tc, :], start=(tc == 0), stop=(tc == NT - 1)),
                          reads=[Cp.res, zb.res], writes=[prr], mark=(tc == NT - 1))
                pim, pir = kb.bank()
                for tc in range(NT):
                    kb.op("pe", lambda e: e.matmul(pim, lhsT=Sp[:, tc, :], rhs=zb[:, tc, :], start=(tc == 0), stop=(tc == NT - 1)),
                          reads=[Sp.res, zb.res], writes=[pir], mark=(tc == NT - 1))
                kb.op("dve", lambda e: e.tensor_tensor(out=t1[:], in0=pre, in1=Ht[:, 0, :], op=ALU.mult), reads=[prr, Ht.res], writes=[t1.res])
                kb.op("dve", lambda e: e.tensor_tensor(out=t2[:], in0=pim, in1=Ht[:, 1, :], op=ALU.mult), reads=[pir, Ht.res], writes=[t2.res])
                kb.op("pool", lambda e: e.tensor_tensor(out=Y[:, j, 0, :], in0=t1[:], in1=t2[:], op=ALU.subtract),
                      reads=[t1.res, t2.res], writes=[Y.res], acc=(j > 0))
                kb.op("dve", lambda e: e.tensor_tensor(out=t3[:], in0=pim, in1=Ht[:, 0, :], op=ALU.mult), reads=[pir, Ht.res], writes=[t3.res])
                kb.op("dve", lambda e: e.tensor_tensor(out=t4[:], in0=pre, in1=Ht[:, 1, :], op=ALU.mult), reads=[prr, Ht.res], writes=[t4.res])
                kb.op("pool", lambda e: e.tensor_tensor(out=Y[:, j, 1, :], in0=t3[:], in1=t4[:], op=ALU.add),
                      reads=[t3.res, t4.res], writes=[Y.res], acc=True)
            for i in range(NT):
                Cp, Sp = Cps[pi % 2], Sps[pi % 2]
                zt, gt, ost = zts[i % 2], gts[i % 2], osts[i % 2]
                pi += 1
                rows = slice(r0 + i * 128, r0 + (i + 1) * 128)
                kb.dma("sync", Cp[:], dft.ap[0, i], Cp.sem, writes=[Cp.res], acc=False)
                kb.dma("sync", Sp[:], dft.ap[1, i], Sp.sem, writes=[Sp.res], acc=False)
                if o == 0:
                    kb.dma("sync", zt[:], U.ap[rows, 0:512], zt.sem, reads=[U.res], writes=[zt.res], acc=False)
                else:
                    kb.dma("sync", zt[:], Z1.ap[rows, :], zt.sem, reads=[Z1.res], writes=[zt.res], acc=False)
                kb.dma("sync", gt[:], U.ap[rows, 512 * (o + 1):512 * (o + 2)], gt.sem, reads=[U.res], writes=[gt.res], acc=False)
                py, pyr = kb.bank()
                k = 0
                for fc in range(nf):
                    for ri, pan in enumerate((Cp, Sp)):
                        kb.op("pe", lambda e: e.matmul(py, lhsT=pan[:, fc, :], rhs=Y[:, fc, ri, :], start=(k == 0), stop=(k == 2 * nf - 1)),
                              reads=[pan.res, Y.res], writes=[pyr], mark=(k == 2 * nf - 1))
                        k += 1
                kb.op("dve", lambda e: e.tensor_tensor(out=t1[:], in0=zt[:], in1=skip[:, o, :], op=ALU.mult),
                      reads=[zt.res, skip.res], writes=[t1.res])
                kb.op("dve", lambda e: e.tensor_tensor(out=t2[:], in0=py, in1=t1[:], op=ALU.add), reads=[pyr, t1.res], writes=[t2.res])
                kb.op("pool", lambda e: e.tensor_tensor(out=ost[:], in0=t2[:], in1=gt[:], op=ALU.mult),
                      reads=[t2.res, gt.res], writes=[ost.res])
                if o == 0:
                    kb.dma("sync", Z1.ap[rows, :], ost[:], ost.sem, reads=[ost.res], writes=[Z1.res])
                else:
                    kb.dma("sync", OM.ap[rows, 1024:1536], ost[:], ost.sem, reads=[ost.res], writes=[OM.res])
        self.stage_end(es)

    def layernorm(self, LN, r, y, g, b):
        kb = self.kb
        st, mv, sd, epsb = LN["st"], LN["mv"], LN["sd"], LN["epsb"]
        for c in range(4):
            kb.op("dve", lambda e: e.bn_stats(out=st[:, c, :], in_=r[:, c * 512:(c + 1) * 512]), reads=[r.res], writes=[st.res], acc=(c > 0))
        kb.op("dve", lambda e: e.bn_aggr(out=mv[:], in_=st[:].rearrange("p c s -> p (c s)")), reads=[st.res], writes=[mv.res])
        kb.op("act", lambda e: e.activation(out=sd[:], in_=mv[:, 1:2], func=AF.Sqrt, bias=epsb[:]), reads=[mv.res, epsb.res], writes=[sd.res])
        kb.op("dve", lambda e: e.reciprocal(out=sd[:], in_=sd[:]), reads=[sd.res], writes=[sd.res])
        kb.op("dve", lambda e: e.tensor_scalar(out=r[:], in0=r[:], scalar1=mv[:, 0:1], scalar2=sd[:], op0=ALU.subtract, op1=ALU.mult),
              reads=[r.res, mv.res, sd.res], writes=[r.res])
        kb.op("pool", lambda e: e.tensor_tensor(out=y[:], in0=r[:], in1=g[:], op=ALU.mult), reads=[r.res, g.res], writes=[y.res])
        kb.op("pool", lambda e: e.tensor_tensor(out=y[:], in0=y[:], in1=b[:], op=ALU.add), reads=[y.res, b.res], writes=[y.res])

    def ln_ctx(self, es):
        kb = self.kb
        LN = {"st": Tile(kb, es, "ln_st", [128, 4, 6], F32), "mv": Tile(kb, es, "ln_mv", [128, 2], F32),
              "sd": Tile(kb, es, "ln_sd", [128, 1], F32), "epsb": Tile(kb, es, "ln_eps", [128, 1], F32)}
        kb.op("dve", lambda e: e.memset(LN["epsb"][:], EPS), writes=[LN["epsb"].res])
        return LN

    def load_bc(self, es, name, src_ap, width, src_res=None):
        t = Tile(self.kb, es, name, [128, width], F32, dma=True)
        self.kb.dma("sync", t[:], src_ap.broadcast_to([128, width]), t.sem,
                    reads=([src_res] if src_res is not None else []), writes=[t.res])
        return t

    def stage_outproj(self, l, xsrc):
        kb = self.kb
        es = ExitStack()
        keep_ctx = l < DEPTH - 1
        nblk = NTB if keep_ctx else 16
        OM, X1 = self.OMIX[l], self.X1[l]
        LN = self.ln_ctx(es)
        Wo = Tile(kb, es, "Wo", [128, 16, D], BF16, dma=True)
        for j in range(4):
            kb.dma("pool", Wo[:, :, j * 512:(j + 1) * 512],
                   self.w_out.ap[l, :, j * 512:(j + 1) * 512].rearrange("(kc p) n -> p kc n", p=128), Wo.sem, writes=[Wo.res])
        gmf = Tile(kb, es, "gmf", [128, 16], F32)
        self.load_fm(es, "gmfl", self.g_mix.ap[l], 16, gmf[:], gmf.res)
        gts = [self.load_bc(es, "gt1l", self.MODd[l].ap[0:1, 2 * D:3 * D], D, self.MODd[l].res)]
        if keep_ctx:
            gts.append(self.load_bc(es, "gt1c", self.MODd[l].ap[1:2, 2 * D:3 * D], D, self.MODd[l].res))
        lng = self.load_bc(es, "lng", self.ln1_g.ap[l:l + 1, :], D)
        lnb = self.load_bc(es, "lnb", self.ln1_b.ap[l:l + 1, :], D)
        oms = tiles(kb, es, "om", 2, [128, D], F32, dma=True)
        xos = tiles(kb, es, "xo", 2, [128, D], F32, dma=True)
        rs = tiles(kb, es, "rr", 2, [128, D], F32, dma=True)
        UTs = tiles(kb, es, "UTb", 2, [128, 16, 128], BF16)
        sqj = Tile(kb, es, "sqj", [128, 1024], F32)
        ss3 = Tile(kb, es, "ss3", [128, 3], F32)
        sd3 = Tile(kb, es, "sd3", [128, 3], F32)
        groups = ((0, 1024), (1024, 512), (1536, 512))
        for tb in range(nblk):
            om, xo, r, UTb = oms[tb % 2], xos[tb % 2], rs[tb % 2], UTs[tb % 2]
            y = r
            rows = slice(tb * 128, (tb + 1) * 128)
            kb.dma("sync", om[:], OM.ap[rows, :], om.sem, reads=[OM.res], writes=[om.res], acc=False)
            kb.dma("sync", xo[:], xsrc.ap[rows, :], xo.sem, reads=[xsrc.res], writes=[xo.res], acc=False)
            for gi, (c0, w) in enumerate(groups):
                kb.op("act", lambda e: e.activation(out=sqj[:, 0:w], in_=om[:, c0:c0 + w], func=AF.Square, accum_out=ss3[:, gi:gi + 1]),
                      reads=[om.res], writes=[sqj.res, ss3.res], acc=False)
                kb.op("act", lambda e: e.activation(out=sd3[:, gi:gi + 1], in_=ss3[:, gi:gi + 1], func=AF.Sqrt, scale=1.0 / w, bias=LN["epsb"][:]),
                      reads=[ss3.res, LN["epsb"].res], writes=[sd3.res], acc=(gi > 0))
            kb.op("dve", lambda e: e.reciprocal(out=sd3[:], in_=sd3[:]), reads=[sd3.res], writes=[sd3.res])
            for gi, (c0, w) in enumerate(groups):
                kb.op("dve", lambda e: e.tensor_scalar(out=om[:, c0:c0 + w], in0=om[:, c0:c0 + w], scalar1=sd3[:, gi:gi + 1], scalar2=None, op0=ALU.mult),
                      reads=[om.res, sd3.res], writes=[om.res])
            self.build_ut_block(om, UTb, UTb.res, 0, lambda kc: gmf[:, kc:kc + 1], lambda kc: 0.0, [gmf.res])
            gt = gts[0 if tb < 16 else 1]
            for j in range(4):
                pb, pr = kb.bank()
                for kc in range(16):
                    kb.op("pe", lambda e: e.matmul(pb, lhsT=UTb[:, kc, :], rhs=Wo[:, kc, j * 512:(j + 1) * 512], start=(kc == 0), stop=(kc == 15)),
                          reads=[UTb.res, Wo.res], writes=[pr], mark=(kc == 15))
                kb.op("dve", lambda e: e.tensor_tensor(out=r[:, j * 512:(j + 1) * 512], in0=pb, in1=gt[:, j * 512:(j + 1) * 512], op=ALU.mult),
                      reads=[pr, gt.res], writes=[r.res], acc=(j > 0))
            kb.op("dve", lambda e: e.scalar_tensor_tensor(out=r[:], in0=xo[:], scalar=DN_ALPHA, in1=r[:], op0=ALU.mult, op1=ALU.add),
                  reads=[xo.res, r.res], writes=[r.res])
            self.layernorm(LN, r, y, lng, lnb)
            kb.dma("sync", X1.ap[rows, :], y[:], y.sem, reads=[y.res], writes=[X1.res])
        self.stage_end(es)

    def stage_ffn_up(self, l):
        kb = self.kb
        es = ExitStack()
        keep_ctx = l < DEPTH - 1
        nblk = NTB if keep_ctx else 16
        ntok = nblk * 128
        X1, HT = self.X1[l], self.HT[l]
        MF, MF1 = self.load_modf(es, l)
        UT = Tile(kb, es, "UT2", [128, 16, T], BF16)
        xts = tiles(kb, es, "xt2", 2, [128, D], F32, dma=True)
        utres = [Res() for _ in range(NTB)]
        for tb in range(nblk):
            xt = xts[tb % 2]
            kb.dma("sync", xt[:], X1.ap[tb * 128:(tb + 1) * 128, :], xt.sem, reads=[X1.res], writes=[xt.res], acc=False)
            r = 0 if tb < 16 else 1
            self.build_ut_block(xt, UT, utres[tb], tb * 128,
                                lambda kc: MF1[:, r, 64 + kc:65 + kc], lambda kc: MF[:, r, 48 + kc:49 + kc],
                                [MF.res, MF1.res])
        wp = tiles(kb, es, "w1p", 3, [128, 16, 512], BF16, dma=True)
        r32 = tiles(kb, es, "r32", 3, [128, 512], F32)
        hst = tiles(kb, es, "hst", 4, [128, 512], BF16, dma=True)
        it = 0
        for jp in range(DFF // 512):
            w = wp[jp % 3]
            kb.dma("pool", w[:], self.w1.ap[l, :, jp * 512:(jp + 1) * 512].rearrange("(kc p) n -> p kc n", p=128),
                   w.sem, writes=[w.res], acc=False)
            for fc in range(4):
                for t0 in range(0, ntok, 512):
                    tl = min(512, ntok - t0)
                    ur = [utres[b] for b in range(t0 // 128, (t0 + tl) // 128)]
                    pb, pr = kb.bank()
                    for kc in range(16):
                        kb.op("pe", lambda e: e.matmul(pb[:, 0:tl], lhsT=w[:, kc, fc * 128:(fc + 1) * 128], rhs=UT[:, kc, t0:t0 + tl],
                                                       start=(kc == 0), stop=(kc == 15)),
                              reads=ur + [w.res], writes=[pr], mark=(kc == 15))
                    rr, hs = r32[it % 3], hst[it % 4]
                    it += 1
                    kb.op("act", lambda e: e.activation(out=rr[:, 0:tl], in_=pb[:, 0:tl], func=AF.Relu), reads=[pr], writes=[rr.res])
                    kb.op("pool", lambda e: e.tensor_tensor(out=hs[:, 0:tl], in0=rr[:, 0:tl], in1=rr[:, 0:tl], op=ALU.mult),
                          reads=[rr.res], writes=[hs.res])
                    f0 = jp * 512 + fc * 128
                    kb.dma("sync", HT.ap[f0:f0 + 128, t0:t0 + tl], hs[:, 0:tl], hs.sem, reads=[hs.res], writes=[HT.res])
        self.stage_end(es)

    def stage_ffn_down(self, l):
        kb = self.kb
        es = ExitStack()
        keep_ctx = l < DEPTH - 1
        last = l == DEPTH - 1
        nblk = NTB if keep_ctx else 16
        ntok = nblk * 128
        X1, X2, HT = self.X1[l], self.X2[l], self.HT[l]
        LN = self.ln_ctx(es)
        lng = self.load_bc(es, "lng2", self.ln2_g.ap[l:l + 1, :], D)
        lnb = self.load_bc(es, "lnb2", self.ln2_b.ap[l:l + 1, :], D)
        gt = Tile(kb, es, "gt2", [128, D], F32, dma=True)
        HTg = Tile(kb, es, "HTg", [128, 64, 512], BF16, dma=True)
        wp = tiles(kb, es, "w2p", 2, [128, 16, 512], BF16, dma=True)
        rs = tiles(kb, es, "r2", 4, [128, D], F32, dma=True)
        x1t = Tile(kb, es, "x1t", [128, D], F32, dma=True)
        wi = 0
        yi = 0
        cur_gt = None
        for t0 in range(0, ntok, 512):
            tl = min(512, ntok - t0)
            nm = tl // 128
            row = 0 if t0 < NLAT else 1
            if cur_gt != row:
                kb.dma("sync", gt[:], self.MODd[l].ap[row:row + 1, 5 * D:6 * D].broadcast_to([128, D]), gt.sem,
                       reads=[self.MODd[l].res], writes=[gt.res], acc=False)
                cur_gt = row
            for kq in range(4):
                kb.dma("sync", HTg[:, kq * 16:(kq + 1) * 16, 0:tl],
                       HT.ap[kq * 2048:(kq + 1) * 2048, t0:t0 + tl].rearrange("(kc p) t -> p kc t", p=128), HTg.sem,
                       reads=[HT.res], writes=[HTg.res], acc=(kq > 0))
            for j in range(4):
                bks = [kb.bank() for _ in range(nm)]
                for kq in range(4):
                    w = wp[wi % 2]
                    wi += 1
                    kb.dma("pool", w[:], self.w2.ap[l, kq * 2048:(kq + 1) * 2048, j * 512:(j + 1) * 512].rearrange("(kc p) n -> p kc n", p=128),
                           w.sem, writes=[w.res], acc=False)
                    for m in range(nm):
                        pb, pr = bks[m]
                        for kc in range(16):
                            kb.op("pe", lambda e: e.matmul(pb, lhsT=HTg[:, kq * 16 + kc, m * 128:(m + 1) * 128], rhs=w[:, kc, :],
                                                           start=(kq == 0 and kc == 0), stop=(kq == 3 and kc == 15)),
                                  reads=[HTg.res, w.res], writes=[pr], mark=(kc == 15))
                for m in range(nm):
                    pb, pr = bks[m]
                    kb.op("dve", lambda e: e.tensor_tensor(out=rs[m][:, j * 512:(j + 1) * 512], in0=pb, in1=gt[:, j * 512:(j + 1) * 512], op=ALU.mult),
                          reads=[pr, gt.res], writes=[rs[m].res], acc=(j > 0))
            for m in range(nm):
                r = rs[m]
                y = r
                rows = slice(t0 + m * 128, t0 + (m + 1) * 128)
                kb.dma("sync", x1t[:], X1.ap[rows, :], x1t.sem, reads=[X1.res], writes=[x1t.res], acc=False)
                kb.op("dve", lambda e: e.scalar_tensor_tensor(out=r[:], in0=x1t[:], scalar=DN_ALPHA, in1=r[:], op0=ALU.mult, op1=ALU.add),
                      reads=[x1t.res, r.res], writes=[r.res])
                self.layernorm(LN, r, y, lng, lnb)
                if last:
                    kb.dma("sync", self.yout.ap[rows, :], y[:], y.sem, reads=[y.res], writes=[self.yout.res])
                else:
                    kb.dma("sync", X2.ap[rows, :], y[:], y.sem, reads=[y.res], writes=[X2.res])
        self.stage_end(es)

    def layer_stages(self, l):
        xsrc = self.xin if l == 0 else self.X2[l - 1]
        keep_ctx = l < DEPTH - 1
        self.stage_inproj(l, xsrc)
        self.stage_qkprep(l)
        self.stage_gqa(l)
        self.stage_nat(l)
        self.stage_hyshort(l)
        self.stage_hyfilter(l, False)
        self.stage_hyconv(l, False)
        if keep_ctx:
            self.stage_hyfilter(l, True)
            self.stage_hyconv(l, True)
        self.stage_outproj(l, xsrc)
        self.stage_ffn_up(l)
        self.stage_ffn_down(l)

    def build(self, stages):
        self.declare()
        self.setup_globals()
        for s in stages:
            s(self)
        self.kb.barrier()
        self.glob.close()
        return self.nc


def _bf16(a):
    import ml_dtypes
    return np.asarray(a, dtype=np.float32).astype(ml_dtypes.bfloat16)


def _dft_blocked(n_chunks, L):
    n = n_chunks * 128
    idx = np.arange(n, dtype=np.int64)
    prod = (idx[:, None] * idx[None, :]) % L
    ang = prod.astype(np.float64) * (2.0 * np.pi / L)
    out = np.empty((2, n_chunks, 128, n_chunks, 128), dtype=np.float32)
    for k, tab in enumerate((np.cos(ang), np.sin(ang))):
        out[k] = tab.reshape(n_chunks, 128, n_chunks, 128).transpose(2, 1, 0, 3)
    return _bf16(out)


def _hy_feat(n):
    t = np.linspace(0.0, 1.0, n, dtype=np.float32)
    bands = np.arange(1, 17, dtype=np.float32)
    ang = (2.0 * np.float32(math.pi) * t[:, None] * bands[None, :]).astype(np.float32)
    feat = np.concatenate([t[:, None], np.cos(ang), np.sin(ang)], -1).astype(np.float32)
    max_decay = math.log(1e-2) / 0.3
    min_decay = math.log(1e-2) / 1.5
    deltas = np.abs(np.linspace(min_decay, max_decay, 512, dtype=np.float32))
    win = (np.exp(-t[:, None] * deltas[None, :]) + np.float32(0.05)).astype(np.float32)
    return np.ascontiguousarray(feat.T), win


def _nat_index():
    rows = NLAT // GRID_W
    pats = {0: 0, 1: 1, 2: 5, 3: 14, 4: 15}
    valid = np.zeros((5, 128, 640), dtype=bool)
    drow = np.zeros((5, 128, 640), dtype=np.int64)
    dcol = np.zeros((5, 128, 640), dtype=np.int64)
    for pi, j in pats.items():
        wb0 = min(max(j - 2, 0), 11)
        for q in range(128):
            tq = j * 128 + q
            r, c = tq // GRID_W, tq % GRID_W
            r0 = min(max(r - 4, 0), rows - 8)
            c0 = min(max(c - 8, 0), GRID_W - 16)
            for k in range(640):
                tk = wb0 * 128 + k
                kr, kc = tk // GRID_W, tk % GRID_W
                if r0 <= kr < r0 + 8 and c0 <= kc < c0 + 16:
                    valid[pi, q, k] = True
                    drow[pi, q, k] = kr - r + 7
                    dcol[pi, q, k] = min(max(kc - c + 15, 0), 30)
    return valid, drow, dcol


_CONST_CACHE = {}


def make_consts(nat_rpb):
    if "c" not in _CONST_CACHE:
        c = {}
        c["c_ident"] = np.eye(128, dtype=np.float32)
        t = np.arange(NLAT)
        row = (t // GRID_W).astype(np.float32)
        col = (t % GRID_W).astype(np.float32)
        inv = (np.float32(10000.0) ** (-np.arange(32, dtype=np.float32) / np.float32(32))).astype(np.float32)
        ang = np.concatenate([row[:, None] * inv[None, :], col[:, None] * inv[None, :]], -1).astype(np.float32)
        c["c_rope"] = np.stack([np.cos(ang), np.sin(ang)]).astype(np.float32)
        c["c_dft"] = _dft_blocked(NF, 4096)
        c["c_dftc"] = _dft_blocked(NFC, 512)
        wf = np.zeros((2, NF * 128), dtype=np.float32)
        wf[0, :2049] = 2.0 / 4096
        wf[0, 0] = wf[0, 2048] = 1.0 / 4096
        wf[1, :257] = 2.0 / 512
        wf[1, 0] = wf[1, 256] = 1.0 / 512
        c["c_wf"] = np.ascontiguousarray(wf.reshape(2, NF, 128).transpose(0, 2, 1))
        c["c_feat"], c["c_win"] = _hy_feat(NLAT)
        c["c_featc"], c["c_winc"] = _hy_feat(NCTX)
        c["_nat"] = _nat_index()
        _CONST_CACHE["c"] = c
    c = dict(_CONST_CACHE["c"])
    valid, drow, dcol = c.pop("_nat")
    rpb = np.asarray(nat_rpb, dtype=np.float32)
    g = rpb[:, :, drow, dcol]
    c["c_nbias"] = np.where(valid[None, None], g, np.float32(NEG)).astype(np.float32)
    return c


def core_inputs(inputs, b, consts):
    m = {}
    m["xin"] = np.ascontiguousarray(np.concatenate([inputs["x"][b], inputs["ctx"][b]], 0), dtype=np.float32)
    m["cvec"] = np.ascontiguousarray(np.stack([inputs["c"][b], inputs["c_ctx"]]), dtype=np.float32)
    for k in ("w_mod", "b_mod", "w_in", "q_norm_g", "k_norm_g", "hy_short_w", "hy_short_b", "hf_w1", "hf_b1",
              "hf_freq", "hf_w2", "hf_b2", "hf_w3", "hy_bias", "g_mix", "w_out", "ln1_g", "ln1_b", "w1", "w2",
              "ln2_g", "ln2_b"):
        m[k] = np.ascontiguousarray(inputs[k], dtype=np.float32)
    m.update(consts)
    return m


_PROG_CACHE = {}


def _get_prog():
    if "p" not in _PROG_CACHE:
        p = Prog(dbg=None)
        p.build([lambda q: q.stage_mod()] + [(lambda q, l=l: q.layer_stages(l)) for l in range(DEPTH)])
        _PROG_CACHE["p"] = p
    return _PROG_CACHE["p"]


def kernel(**inputs):
    inputs = {k: np.asarray(v) for k, v in inputs.items()}
    consts = make_consts(inputs["nat_rpb"])
    p = _get_prog()
    n = 8
    in_maps = []
    for b in range(n):
        m = core_inputs(inputs, b, consts)
        in_maps.append({k: v for k, v in m.items() if k in p.inputs})
    res = run_bass_kernel_spmd(p.nc, in_maps, core_ids=list(range(n)))
    out = np.stack([np.asarray(r["yout"], dtype=np.float32) for r in res.results], axis=0)
    return out
```

```python
import math
import numpy as np
from contextlib import ExitStack
import concourse.bass as bass
import concourse.mybir as mybir
from concourse.alu_op_type import AluOpType as ALU
from concourse.bass_utils import run_bass_kernel_spmd

F32 = mybir.dt.float32
BF16 = mybir.dt.bfloat16
AF = mybir.ActivationFunctionType
AX = mybir.AxisListType

D = 2048
NLAT = 2048
NCTX = 256
T = NLAT + NCTX
NTB = T // 128
DIN = 4608
DFF = 8192
DEPTH = 2
HD = 128
EPS = 1e-6
DN_ALPHA = (2 * DEPTH) ** 0.25
GRID_W = 64
NEG = -30000.0
NF = 17
NFC = 3
TWO_PI = 2.0 * math.pi

C_AQ, C_AK, C_AV, C_HY, C_NQ, C_NK, C_NV = 0, 1024, 1280, 1536, 3072, 3584, 4096


class Sem:
    __slots__ = ("h", "count", "dma")

    def __init__(self, h, dma):
        self.h = h
        self.count = 0
        self.dma = dma


class Res:
    __slots__ = ("w", "r", "rp")

    def __init__(self):
        self.w = {}
        self.r = {}
        self.rp = {}


class Eng:
    def __init__(self, kb, name, e):
        self.kb = kb
        self.name = name
        self.e = e
        self.sem = kb.new_sem(name, False)
        self.waited = {}

    def wait(self, sem, val):
        if sem.dma:
            val = sem.count
        if val <= 0:
            return
        if sem is self.sem and self.name == "pe":
            return
        k = id(sem)
        if self.waited.get(k, 0) >= val:
            return
        self.waited[k] = val
        self.e.wait_ge(sem.h, val)


class KB:
    def __init__(self, nc):
        self.nc = nc
        self.nsem = 0
        self.E = {}
        for name, e in (("pe", nc.tensor), ("act", nc.scalar), ("dve", nc.vector),
                        ("pool", nc.gpsimd), ("sync", nc.sync)):
            self.E[name] = Eng(self, name, e)
        self.dsems = []
        self.dcur = 0
        self.twins = {}
        self.old_twins = []
        self.psum = None
        self.banks = []
        self.bank_i = 0
        self.obank_i = 0

    def new_sem(self, name, dma):
        self.nsem += 1
        h = self.nc.alloc_semaphore(name=f"{name}_{self.nsem}")
        return Sem(h, dma)

    def dsem(self):
        if self.dcur == len(self.dsems):
            self.dsems.append(self.new_sem("dma", True))
        s = self.dsems[self.dcur]
        if s.count > 30000:
            s = self.new_sem("dma", True)
            self.dsems[self.dcur] = s
        self.dcur += 1
        return s

    def _waits(self, E, reads, writes, acc):
        for r in reads:
            for s, v in r.w.items():
                E.wait(s, v)
        for w in writes:
            if acc:
                for s, v in w.rp.items():
                    E.wait(s, v)
            else:
                for s, v in w.w.items():
                    E.wait(s, v)
                for s, v in w.r.items():
                    E.wait(s, v)

    def _mark(self, S, reads, writes, acc):
        for r in reads:
            r.r[S] = S.count
        for w in writes:
            if acc:
                w.w[S] = S.count
            else:
                w.rp = w.r
                w.w = {S: S.count}
                w.r = {}

    def op(self, eng, fn, reads=(), writes=(), mark=True, acc=False):
        E = self.E[eng]
        self._waits(E, reads, writes, acc)
        ins = fn(E.e)
        if mark:
            S = E.sem
            S.count += 1
            ins.then_inc(S.h, 1)
            self._mark(S, reads, writes, acc)
            if S.count > 30000:
                E.sem = self.new_sem(E.name, False)
        return ins

    def sw_twin(self, sem):
        t = self.twins.get(id(sem))
        if t is None or t.count > 30000:
            if t is not None:
                self.old_twins.append(t)
            t = self.new_sem("swdma", True)
            self.twins[id(sem)] = t
        return t

    def dma(self, q, out, in_, sem, reads=(), writes=(), acc=True, **kw):
        E = self.E[q]
        if q == "pool":
            sem = self.sw_twin(sem)
        self._waits(E, reads, writes, acc)
        ins = E.e.dma_start(out=out, in_=in_, **kw)
        sem.count += 16
        ins.then_inc(sem.h, 16)
        self._mark(sem, reads, writes, acc)
        return ins

    def barrier(self):
        sems = [E.sem for E in self.E.values()] + list(self.dsems) + list(self.twins.values()) + self.old_twins
        self.old_twins = []
        for E in self.E.values():
            for s in sems:
                if s is E.sem:
                    continue
                E.wait(s, s.count)
        self.dcur = 0

    def bank(self, n=8):
        i = self.bank_i % n
        self.bank_i = (i + 1) % n
        return self.psum[:, i, :], self.banks[i]

    def obank(self):
        i = 6 + (self.obank_i % 2)
        self.obank_i += 1
        return self.psum[:, i, :], self.banks[i]


class DT:
    def __init__(self, nc, name, shape, dtype, kind="Internal"):
        self.t = nc.dram_tensor(name, list(shape), dtype, kind=kind)
        self.ap = self.t.ap()
        self.res = Res()
        self.shape = shape


class Tile:
    def __init__(self, kb, es, name, shape, dtype, dma=False):
        kb.ntile = getattr(kb, "ntile", 0) + 1
        self.t = es.enter_context(kb.nc.sbuf_tensor(f"{name}_{kb.ntile}", list(shape), dtype))
        self.res = Res()
        self.sem = kb.dsem() if dma else None

    def __getitem__(self, idx):
        return self.t[idx]


def tiles(kb, es, name, n, shape, dtype, dma=False):
    return [Tile(kb, es, f"{name}{i}", shape, dtype, dma) for i in range(n)]


class Prog:
    def __init__(self, dbg=False, n_layers=DEPTH, stop_after=None):
        self.dbg = dbg
        self.n_layers = n_layers
        self.stop_after = stop_after
        nc = bass.Bass("TRN2", target_bir_lowering=False)
        self.nc = nc
        self.kb = KB(nc)
        self.glob = ExitStack()
        self.inputs = {}
        self.scratch = {}
        self.outputs = {}

    def inp(self, name, shape, dtype=F32):
        d = DT(self.nc, name, shape, dtype, kind="ExternalInput")
        self.inputs[name] = d
        return d

    def scr(self, name, shape, dtype=F32):
        kind = "ExternalOutput" if (self.dbg and name in self.dbg) else "Internal"
        d = DT(self.nc, name, shape, dtype, kind=kind)
        self.scratch[name] = d
        return d

    def declare(self):
        L = DEPTH
        i = self.inp
        self.xin = i("xin", [T, D])
        self.cvec = i("cvec", [2, D])
        self.w_mod = i("w_mod", [L, D, 6 * D])
        self.b_mod = i("b_mod", [L, 6 * D])
        self.w_in = i("w_in", [L, D, DIN])
        self.q_norm_g = i("q_norm_g", [L, HD])
        self.k_norm_g = i("k_norm_g", [L, HD])
        self.hy_short_w = i("hy_short_w", [L, 3, 1536])
        self.hy_short_b = i("hy_short_b", [L, 1536])
        self.hf_w1 = i("hf_w1", [L, 33, 64])
        self.hf_b1 = i("hf_b1", [L, 64])
        self.hf_freq = i("hf_freq", [L, 64])
        self.hf_w2 = i("hf_w2", [L, 64, 64])
        self.hf_b2 = i("hf_b2", [L, 64])
        self.hf_w3 = i("hf_w3", [L, 64, 2048])
        self.hy_bias = i("hy_bias", [L, 2, 512])
        self.g_mix = i("g_mix", [L, D])
        self.w_out = i("w_out", [L, D, D])
        self.ln1_g = i("ln1_g", [L, D])
        self.ln1_b = i("ln1_b", [L, D])
        self.w1 = i("w1", [L, D, DFF])
        self.w2 = i("w2", [L, DFF, D])
        self.ln2_g = i("ln2_g", [L, D])
        self.ln2_b = i("ln2_b", [L, D])
        self.c_ident = i("c_ident", [128, 128])
        self.c_rope = i("c_rope", [2, NLAT, 64])
        self.c_nbias = i("c_nbias", [L, 4, 5, 128, 640])
        self.c_dft = i("c_dft", [2, NF, 128, NF, 128], BF16)
        self.c_dftc = i("c_dftc", [2, NFC, 128, NFC, 128], BF16)
        self.c_wf = i("c_wf", [2, 128, NF])
        self.c_feat = i("c_feat", [33, NLAT])
        self.c_featc = i("c_featc", [33, NCTX])
        self.c_win = i("c_win", [NLAT, 512])
        self.c_winc = i("c_winc", [NCTX, 512])
        self.yout = DT(self.nc, "yout", [NLAT, D], F32, kind="ExternalOutput")
        s = self.scr
        self.MODd = [s(f"MODd{l}", [2, 6 * D]) for l in range(L)]
        self.P = [s(f"P{l}", [T, DIN]) for l in range(L)]
        self.QT = [s(f"QT{l}", [10, 128, T], BF16) for l in range(L)]
        self.NQT = [s(f"NQT{l}", [8, 128, T], BF16) for l in range(L)]
        self.U = [s(f"U{l}", [T, 1536]) for l in range(L)]
        self.HS = [s(f"HS{l}", [NF * 128, 4, 512]) for l in range(L)]
        self.HSC = [s(f"HSC{l}", [NFC * 128, 4, 512]) for l in range(L)]
        self.Z1 = [s(f"Z1_{l}", [T, 512]) for l in range(L)]
        self.OMIX = [s(f"OMIX{l}", [T, D]) for l in range(L)]
        self.X1 = [s(f"X1_{l}", [T, D]) for l in range(L)]
        self.X2 = [s(f"X2_{l}", [T, D]) for l in range(L)]
        self.HT = [s(f"HT{l}", [DFF, T], BF16) for l in range(L)]

    def setup_globals(self):
        kb, nc, es = self.kb, self.nc, self.glob
        ps = es.enter_context(nc.psum_tensor("psum", [128, 8, 512], F32))
        kb.psum = ps
        kb.banks = [Res() for _ in range(8)]
        self.ident = Tile(kb, es, "ident", [128, 128], F32, dma=True)
        self.identb = Tile(kb, es, "identb", [128, 128], BF16)
        self.ones = Tile(kb, es, "ones", [128, 128], BF16)
        kb.dma("sync", self.ident[:], self.c_ident.ap, self.ident.sem, writes=[self.ident.res])
        kb.op("dve", lambda e: e.tensor_copy(out=self.identb[:], in_=self.ident[:]),
              reads=[self.ident.res], writes=[self.identb.res])
        kb.op("dve", lambda e: e.memset(self.ones[:], 1.0), writes=[self.ones.res])
        kb.barrier()
        kb.dcur = 1

    def load_fm(self, es, name, vec_ap, n, out_ap, out_res, acc=False, src_res=None):
        kb = self.kb
        stg = Tile(kb, es, name + "_stg", [n, 128], F32, dma=True)
        kb.dma("sync", stg[:], vec_ap.rearrange("(j p) -> j p", p=128), stg.sem,
               reads=([src_res] if src_res is not None else []), writes=[stg.res], acc=False)
        pb, pr = kb.bank()
        kb.op("pe", lambda e: e.transpose(out=pb[:, 0:n], in_=stg[:], identity=self.ident[0:n, 0:n]),
              reads=[stg.res, self.ident.res], writes=[pr])
        kb.op("dve", lambda e: e.tensor_copy(out=out_ap, in_=pb[:, 0:n]), reads=[pr], writes=[out_res], acc=acc)

    def split_bf16(self, src_ap, src_res, hi_ap, lo_ap, dst_res, acc=False):
        kb = self.kb
        kb.op("dve", lambda e: e.tensor_copy(out=hi_ap, in_=src_ap), reads=[src_res], writes=[dst_res], acc=acc)
        kb.op("dve", lambda e: e.tensor_tensor(out=lo_ap, in0=src_ap, in1=hi_ap, op=ALU.subtract),
              reads=[src_res, dst_res], writes=[dst_res], acc=True)

    def mm3(self, pb, pr, lhi, llo, rhi, rlo, reads):
        kb = self.kb
        kb.op("pe", lambda e: e.matmul(pb, lhsT=lhi, rhs=rhi, start=True, stop=False), reads=reads, writes=[pr], mark=False)
        kb.op("pe", lambda e: e.matmul(pb, lhsT=lhi, rhs=rlo, start=False, stop=False), reads=reads, writes=[pr], mark=False)
        kb.op("pe", lambda e: e.matmul(pb, lhsT=llo, rhs=rhi, start=False, stop=True), reads=reads, writes=[pr])

    def stage_end(self, es):
        self.kb.barrier()
        es.close()
        self.kb.dcur = 1

    def stage_mod(self):
        kb, nc = self.kb, self.nc
        es = ExitStack()
        cT = Tile(kb, es, "cT", [128, 16, 2], F32)
        sT = Tile(kb, es, "sT", [128, 16, 2], BF16)
        for r in range(2):
            self.load_fm(es, f"cfm{r}", self.cvec.ap[r], 16, cT[:, :, r], cT.res, acc=(r > 0))
        kb.op("act", lambda e: e.activation(out=sT[:], in_=cT[:], func=AF.Silu),
              reads=[cT.res], writes=[sT.res])
        wp = tiles(kb, es, "wmp", 3, [128, 16, 512], BF16, dma=True)
        bm = Tile(kb, es, "bm", [2, 6 * D], F32, dma=True)
        mo = Tile(kb, es, "mo", [2, 6 * D], F32, dma=True)
        it = 0
        for l in range(self.n_layers):
            kb.dma("sync", bm[:], self.b_mod.ap[l:l + 1, :].broadcast_to([2, 6 * D]), bm.sem,
                   writes=[bm.res], acc=False)
            for j in range(24):
                w = wp[it % 3]
                it += 1
                kb.dma("pool", w[:], self.w_mod.ap[l, :, j * 512:(j + 1) * 512].rearrange("(kc p) n -> p kc n", p=128),
                       w.sem, writes=[w.res], acc=False)
                pb, pr = kb.bank()
                for kc in range(16):
                    kb.op("pe", lambda e, kc=kc: e.matmul(pb[0:2, :], lhsT=sT[:, kc, :], rhs=w[:, kc, :],
                                                          start=(kc == 0), stop=(kc == 15)),
                          reads=[sT.res, w.res], writes=[pr], mark=(kc == 15))
                kb.op("dve", lambda e: e.tensor_tensor(out=mo[:, j * 512:(j + 1) * 512], in0=pb[0:2, :],
                                                       in1=bm[:, j * 512:(j + 1) * 512], op=ALU.add),
                      reads=[pr, bm.res], writes=[mo.res], acc=(j > 0))
            kb.dma("sync", self.MODd[l].ap, mo[:], mo.sem, reads=[mo.res], writes=[self.MODd[l].res])
        self.stage_end(es)

    def load_modf(self, es, l):
        kb = self.kb
        MF = Tile(kb, es, "MF", [128, 2, 96], F32)
        MF1 = Tile(kb, es, "MF1", [128, 2, 96], F32)
        for r in range(2):
            self.load_fm(es, f"mfm{r}", self.MODd[l].ap[r], 96, MF[:, r, :], MF.res, acc=(r > 0),
                         src_res=self.MODd[l].res)
        kb.op("dve", lambda e: e.tensor_scalar(out=MF1[:], in0=MF[:], scalar1=1.0, scalar2=None, op0=ALU.add),
              reads=[MF.res], writes=[MF1.res])
        return MF, MF1

    def build_ut_block(self, xt, UT, ut_res, col0, scale_fn, bias_fn, extra_reads):
        kb = self.kb
        for g in range(4):
            pb, pr = kb.bank()
            for i in range(4):
                kc = g * 4 + i
                kb.op("pe", lambda e, kc=kc, i=i: e.transpose(out=pb[:, i * 128:(i + 1) * 128],
                                                              in_=xt[:, kc * 128:(kc + 1) * 128],
                                                              identity=self.ident[:]),
                      reads=[xt.res, self.ident.res], writes=[pr], mark=(i == 3))
            for i in range(4):
                kc = g * 4 + i
                sc, bi = scale_fn(kc), bias_fn(kc)
                kb.op("act", lambda e, kc=kc, i=i, sc=sc, bi=bi: e.activation(
                    out=UT[:, kc, col0:col0 + 128], in_=pb[:, i * 128:(i + 1) * 128],
                    func=AF.Identity, scale=sc, bias=bi),
                    reads=[pr] + extra_reads, writes=[ut_res], acc=True)

    def stage_inproj(self, l, xsrc):
        kb = self.kb
        es = ExitStack()
        MF, MF1 = self.load_modf(es, l)
        UT = Tile(kb, es, "UT", [128, 16, T], BF16)
        xts = tiles(kb, es, "xt", 2, [128, D], F32, dma=True)
        utres = [Res() for _ in range(NTB)]
        for tb in range(NTB):
            xt = xts[tb % 2]
            kb.dma("sync", xt[:], xsrc.ap[tb * 128:(tb + 1) * 128, :], xt.sem, reads=[xsrc.res],
                   writes=[xt.res], acc=False)
            r = 0 if tb < 16 else 1
            self.build_ut_block(xt, UT, utres[tb], tb * 128,
                                lambda kc: MF1[:, r, 16 + kc:17 + kc], lambda kc: MF[:, r, kc:kc + 1],
                                [MF.res, MF1.res])
        wp = tiles(kb, es, "wip", 3, [128, 16, 512], BF16, dma=True)
        st = tiles(kb, es, "pst", 4, [128, 512], F32, dma=True)
        si = 0
        for j in range(DIN // 512):
            w = wp[j % 3]
            kb.dma("pool", w[:], self.w_in.ap[l, :, j * 512:(j + 1) * 512].rearrange("(kc p) n -> p kc n", p=128),
                   w.sem, writes=[w.res], acc=False)
            for tb in range(NTB):
                pb, pr = kb.bank()
                for kc in range(16):
                    kb.op("pe", lambda e, kc=kc: e.matmul(pb, lhsT=UT[:, kc, tb * 128:(tb + 1) * 128], rhs=w[:, kc, :],
                                                          start=(kc == 0), stop=(kc == 15)),
                          reads=[utres[tb], w.res], writes=[pr], mark=(kc == 15))
                s = st[si % 4]
                eng = "act" if si % 2 == 0 else "dve"
                si += 1
                if eng == "act":
                    kb.op("act", lambda e: e.copy(out=s[:], in_=pb), reads=[pr], writes=[s.res])
                else:
                    kb.op("dve", lambda e: e.tensor_copy(out=s[:], in_=pb), reads=[pr], writes=[s.res])
                kb.dma("sync", self.P[l].ap[tb * 128:(tb + 1) * 128, j * 512:(j + 1) * 512], s[:], s.sem,
                       reads=[s.res], writes=[self.P[l].res])
        self.stage_end(es)


    def stage_qkprep(self, l):
        kb = self.kb
        es = ExitStack()
        scale = HD ** -0.5
        G = Tile(kb, es, "G", [128, 2, 128], F32, dma=True)
        kb.dma("sync", G[:, 0, :], self.q_norm_g.ap[l:l + 1, :].broadcast_to([128, 128]), G.sem, writes=[G.res])
        kb.dma("sync", G[:, 1, :], self.k_norm_g.ap[l:l + 1, :].broadcast_to([128, 128]), G.sem, writes=[G.res])
        kb.op("dve", lambda e: e.tensor_scalar(out=G[:, 0, :], in0=G[:, 0, :], scalar1=scale, scalar2=None, op0=ALU.mult),
              reads=[G.res], writes=[G.res])
        t1s = tiles(kb, es, "t1", 2, [128, 1280], F32, dma=True)
        t2s = tiles(kb, es, "t2", 2, [128, 1024], F32, dma=True)
        css = tiles(kb, es, "cs", 2, [128, 2, 64], F32, dma=True)
        sq = Tile(kb, es, "sq", [128, 1280], F32)
        ss = Tile(kb, es, "ss", [128, 10], F32)
        sd = Tile(kb, es, "sd", [128, 10], F32)
        rstd = Tile(kb, es, "rstd", [128, 10], F32)
        xn = Tile(kb, es, "xn", [128, 10, 128], F32)
        ra = Tile(kb, es, "ra", [128, 10, 64], F32)
        rb = Tile(kb, es, "rb", [128, 10, 64], F32)
        rc = Tile(kb, es, "rc", [128, 10, 64], F32)
        rd = Tile(kb, es, "rd", [128, 10, 64], F32)
        xrs = tiles(kb, es, "xr", 2, [128, 18, 128], BF16)
        stgs = tiles(kb, es, "qstg", 2, [128, 18, 128], BF16, dma=True)
        epsb = Tile(kb, es, "epsb", [128, 1], F32)
        kb.op("dve", lambda e: e.memset(epsb[:], EPS), writes=[epsb.res])
        P = self.P[l]
        for tb in range(NTB):
            t1, t2, cs, xr, stg = t1s[tb % 2], t2s[tb % 2], css[tb % 2], xrs[tb % 2], stgs[tb % 2]
            rows = slice(tb * 128, (tb + 1) * 128)
            kb.dma("sync", t1[:], P.ap[rows, 0:1280], t1.sem, reads=[P.res], writes=[t1.res], acc=False)
            kb.dma("sync", t2[:], P.ap[rows, C_NQ:C_NQ + 1024], t2.sem, reads=[P.res], writes=[t2.res], acc=False)
            lat = tb < 16
            if lat:
                kb.dma("sync", cs[:], self.c_rope.ap[:, rows, :].rearrange("c t i -> t c i"), cs.sem,
                       writes=[cs.res], acc=False)
            kb.op("act", lambda e: e.activation(out=sq[:], in_=t1[:], func=AF.Square), reads=[t1.res], writes=[sq.res])
            kb.op("dve", lambda e: e.tensor_reduce(out=ss[:], in_=sq[:].rearrange("p (h d) -> p h d", d=128),
                                                   axis=AX.X, op=ALU.add), reads=[sq.res], writes=[ss.res])
            kb.op("act", lambda e: e.activation(out=sd[:], in_=ss[:], func=AF.Sqrt, scale=1.0 / HD, bias=epsb[:]),
                  reads=[ss.res, epsb.res], writes=[sd.res])
            kb.op("dve", lambda e: e.reciprocal(out=rstd[:], in_=sd[:]), reads=[sd.res], writes=[rstd.res])
            for h in range(10):
                gi = 0 if h < 8 else 1
                kb.op("dve", lambda e: e.scalar_tensor_tensor(out=xn[:, h, :], in0=t1[:, h * 128:(h + 1) * 128],
                                                              scalar=rstd[:, h:h + 1], in1=G[:, gi, :],
                                                              op0=ALU.mult, op1=ALU.mult),
                      reads=[t1.res, rstd.res, G.res], writes=[xn.res], acc=(h > 0))
            if lat:
                xv = xn[:].rearrange("p h (i two) -> p h i two", two=2)
                x0, x1 = xv[:, :, :, 0], xv[:, :, :, 1]
                ov = xr[:, 0:10, :].rearrange("p h (i two) -> p h i two", two=2)
                o0, o1 = ov[:, :, :, 0], ov[:, :, :, 1]
                cb = cs[:, 0, :].unsqueeze(1).broadcast_to([128, 10, 64])
                sb = cs[:, 1, :].unsqueeze(1).broadcast_to([128, 10, 64])
                kb.op("dve", lambda e: e.tensor_tensor(out=ra[:], in0=x0, in1=cb, op=ALU.mult),
                      reads=[xn.res, cs.res], writes=[ra.res])
                kb.op("pool", lambda e: e.tensor_tensor(out=rb[:], in0=x1, in1=sb, op=ALU.mult),
                      reads=[xn.res, cs.res], writes=[rb.res])
                kb.op("pool", lambda e: e.tensor_tensor(out=rc[:], in0=x0, in1=sb, op=ALU.mult),
                      reads=[xn.res, cs.res], writes=[rc.res])
                kb.op("dve", lambda e: e.tensor_tensor(out=rd[:], in0=x1, in1=cb, op=ALU.mult),
                      reads=[xn.res, cs.res], writes=[rd.res])
                kb.op("dve", lambda e: e.tensor_tensor(out=o0, in0=ra[:], in1=rb[:], op=ALU.subtract),
                      reads=[ra.res, rb.res], writes=[xr.res])
                kb.op("pool", lambda e: e.tensor_tensor(out=o1, in0=rc[:], in1=rd[:], op=ALU.add),
                      reads=[rc.res, rd.res], writes=[xr.res], acc=True)
            else:
                kb.op("act", lambda e: e.copy(out=xr[:, 0:10, :], in_=xn[:]), reads=[xn.res], writes=[xr.res])
            kb.op("act", lambda e: e.mul(out=xr[:, 10:14, :], in_=t2[:, 0:512].rearrange("p (h d) -> p h d", d=128), mul=scale),
                  reads=[t2.res], writes=[xr.res], acc=True)
            kb.op("act", lambda e: e.copy(out=xr[:, 14:18, :], in_=t2[:, 512:1024].rearrange("p (h d) -> p h d", d=128)),
                  reads=[t2.res], writes=[xr.res], acc=True)
            for g0 in range(0, 18, 8):
                n = min(8, 18 - g0)
                pb, pr = kb.bank()
                pb16 = pb.bitcast(BF16)
                for i in range(n):
                    kb.op("pe", lambda e: e.transpose(out=pb16[:, i * 128:(i + 1) * 128], in_=xr[:, g0 + i, :],
                                                      identity=self.identb[:]),
                          reads=[xr.res, self.identb.res], writes=[pr], mark=(i == n - 1))
                eng = "act" if (g0 // 8) % 2 == 0 else "dve"
                dst = stg[:, g0:g0 + n, :].rearrange("p h t -> p (h t)")
                if eng == "act":
                    kb.op("act", lambda e: e.copy(out=dst, in_=pb16[:, 0:n * 128]), reads=[pr], writes=[stg.res], acc=(g0 > 0))
                else:
                    kb.op("dve", lambda e: e.tensor_copy(out=dst, in_=pb16[:, 0:n * 128]), reads=[pr], writes=[stg.res], acc=True)
            kb.dma("sync", self.QT[l].ap[:, :, rows].rearrange("h d t -> d h t"), stg[:, 0:10, :], stg.sem,
                   reads=[stg.res], writes=[self.QT[l].res])
            kb.dma("sync", self.NQT[l].ap[:, :, rows].rearrange("h d t -> d h t"), stg[:, 10:18, :], stg.sem,
                   reads=[stg.res], writes=[self.NQT[l].res])
        self.stage_end(es)

    def attn_block(self, A, qT, qcol, kT, V, chunks, use_max, out_ap, out_res):
        A["jobs"].append((qT, qcol, kT, V, chunks, use_max, out_ap, out_res))

    def attn_flush(self, A):
        kb = self.kb
        jobs = A["jobs"]
        A["jobs"] = []
        items = [(bi, ci) for bi, job in enumerate(jobs) for ci in range(len(job[4]))]
        st = {}
        blk = {}

        def emit_s(bi, ci):
            qT, qcol, kT, V, chunks, use_max, out_ap, out_res = jobs[bi]
            q = qT[:, qcol:qcol + 128]
            nch = len(chunks)
            if ci == 0:
                B = {"rs": A["rs"][A["it"] % 4], "nm": A["nm"][A["it"] % 4], "mx": A["mx"][A["it"] % 4],
                     "rsum": A["rsum"][A["it"] % 4], "ost": A["ost"][A["it"] % 4], "si": 0,
                     "nsub": sum(ln // 128 for _, ln, _ in chunks)}
                A["it"] += 1
                blk[bi] = B
                if use_max:
                    for cj, (k0, ln, bias) in enumerate(chunks):
                        pb, pr = kb.bank(6)
                        kb.op("pe", lambda e: e.matmul(pb[:, 0:ln], lhsT=q, rhs=kT[:, k0:k0 + ln], start=True, stop=True),
                              reads=[qT.res, kT.res], writes=[pr])
                        if bias is not None:
                            sbt = A["sb"][A["sbi"] % 3]
                            A["sbi"] += 1
                            kb.op("dve", lambda e: e.tensor_tensor(out=sbt[:, 0:ln], in0=pb[:, 0:ln], in1=bias[0], op=ALU.add),
                                  reads=[pr, bias[1]], writes=[sbt.res])
                            kb.op("dve", lambda e: e.reduce_max(out=B["mx"][:, cj:cj + 1], in_=sbt[:, 0:ln], axis=AX.X),
                                  reads=[sbt.res], writes=[B["mx"].res], acc=(cj > 0))
                        else:
                            kb.op("dve", lambda e: e.reduce_max(out=B["mx"][:, cj:cj + 1], in_=pb[:, 0:ln], axis=AX.X),
                                  reads=[pr], writes=[B["mx"].res], acc=(cj > 0))
                    kb.op("dve", lambda e: e.tensor_reduce(out=B["nm"][:], in_=B["mx"][:, 0:nch], axis=AX.X, op=ALU.max, negate=True),
                          reads=[B["mx"].res], writes=[B["nm"].res])
            B = blk[bi]
            k0, ln, bias = chunks[ci]
            pb, pr = kb.bank(6)
            kb.op("pe", lambda e: e.matmul(pb[:, 0:ln], lhsT=q, rhs=kT[:, k0:k0 + ln], start=True, stop=True),
                  reads=[qT.res, kT.res], writes=[pr])
            src, src_res = pb[:, 0:ln], pr
            if bias is not None:
                sbt = A["sb"][A["sbi"] % 3]
                A["sbi"] += 1
                kb.op("dve", lambda e: e.tensor_tensor(out=sbt[:, 0:ln], in0=pb[:, 0:ln], in1=bias[0], op=ALU.add),
                      reads=[pr, bias[1]], writes=[sbt.res])
                src, src_res = sbt[:, 0:ln], sbt.res
            Pb = A["Pb"][A["pi"] % 4]
            PT = A["PT"][A["pi"] % 4]
            A["pi"] += 1
            rs = B["rs"]
            if use_max:
                kb.op("act", lambda e: e.activation(out=Pb[:, 0:ln], in_=src, func=AF.Exp, bias=B["nm"][:], accum_out=rs[:, ci:ci + 1]),
                      reads=[src_res, B["nm"].res], writes=[Pb.res, rs.res], acc=False)
            else:
                kb.op("act", lambda e: e.activation(out=Pb[:, 0:ln], in_=src, func=AF.Exp, accum_out=rs[:, ci:ci + 1]),
                      reads=[src_res], writes=[Pb.res, rs.res], acc=False)
            st[(bi, ci)] = (Pb, PT)

        def emit_t(bi, ci):
            chunks = jobs[bi][4]
            k0, ln, bias = chunks[ci]
            Pb, PT = st[(bi, ci)]
            nsub = ln // 128
            tb_, tr = kb.bank(6)
            tb16 = tb_.bitcast(BF16)
            for i in range(nsub):
                kb.op("pe", lambda e: e.transpose(out=tb16[:, i * 128:(i + 1) * 128], in_=Pb[:, i * 128:(i + 1) * 128],
                                                  identity=self.identb[:]),
                      reads=[Pb.res, self.identb.res], writes=[tr], mark=(i == nsub - 1))
            kb.op("dve", lambda e: e.tensor_copy(out=PT[:, 0:nsub, :].rearrange("p s t -> p (s t)"), in_=tb16[:, 0:nsub * 128]),
                  reads=[tr], writes=[PT.res])

        def emit_pv(bi, ci):
            qT, qcol, kT, V, chunks, use_max, out_ap, out_res = jobs[bi]
            B = blk[bi]
            k0, ln, bias = chunks[ci]
            Pb, PT = st[(bi, ci)]
            nsub = ln // 128
            if ci == 0:
                B["ob"] = kb.obank()
            ob, orr = B["ob"]
            for i in range(nsub):
                kc = (k0 + i * 128) // 128
                last = (B["si"] == B["nsub"] - 1)
                first = (B["si"] == 0)
                kb.op("pe", lambda e: e.matmul(ob[:, 0:128], lhsT=PT[:, i, :], rhs=V[:, kc, :], start=first, stop=last),
                      reads=[PT.res, V.res], writes=[orr], mark=(last or i == nsub - 1))
                B["si"] += 1
            if ci == len(chunks) - 1:
                nch = len(chunks)
                rs, rsum, ost = B["rs"], B["rsum"], B["ost"]
                kb.op("dve", lambda e: e.reduce_sum(out=rsum[:], in_=rs[:, 0:nch], axis=AX.X), reads=[rs.res], writes=[rsum.res])
                kb.op("dve", lambda e: e.reciprocal(out=rsum[:], in_=rsum[:]), reads=[rsum.res], writes=[rsum.res])
                kb.op("act", lambda e: e.activation(out=ost[:], in_=ob[:, 0:128], func=AF.Identity, scale=rsum[:]),
                      reads=[orr, rsum.res], writes=[ost.res])
                kb.dma("sync", out_ap, ost[:], ost.sem, reads=[ost.res], writes=[out_res])

        n = len(items)
        for s_ in range(n + 2):
            if s_ < n:
                emit_s(*items[s_])
            if 1 <= s_ <= n:
                emit_t(*items[s_ - 1])
            if 2 <= s_ <= n + 1:
                emit_pv(*items[s_ - 2])

    def attn_ctx(self, es):
        kb = self.kb
        A = {"it": 0, "sbi": 0, "pi": 0, "jobs": []}
        A["rs"] = tiles(kb, es, "a_rs", 4, [128, 8], F32)
        A["nm"] = tiles(kb, es, "a_nm", 4, [128, 1], F32)
        A["mx"] = tiles(kb, es, "a_mx", 4, [128, 8], F32)
        A["rsum"] = tiles(kb, es, "a_rsum", 4, [128, 1], F32)
        A["sb"] = tiles(kb, es, "a_sb", 3, [128, 512], F32)
        A["Pb"] = tiles(kb, es, "a_Pb", 4, [128, 512], BF16)
        A["PT"] = tiles(kb, es, "a_PT", 4, [128, 4, 128], BF16)
        A["ost"] = tiles(kb, es, "a_ost", 4, [128, 128], F32, dma=True)
        return A

    def stage_gqa(self, l):
        kb = self.kb
        es = ExitStack()
        A = self.attn_ctx(es)
        keep_ctx = l < DEPTH - 1
        kTs = tiles(kb, es, "kT", 2, [128, T], BF16, dma=True)
        Vs = tiles(kb, es, "V", 2, [128, NTB, 128], BF16, dma=True)
        qTs = tiles(kb, es, "qT", 2, [128, T], BF16, dma=True)
        P, QT, OM = self.P[l], self.QT[l], self.OMIX[l]
        lat_chunks = [(0, 512, None), (512, 512, None), (1024, 512, None), (1536, 512, None), (2048, 256, None)]
        for g in range(2):
            kT, V = kTs[g], Vs[g]
            kb.dma("sync", kT[:], QT.ap[8 + g], kT.sem, reads=[QT.res], writes=[kT.res], acc=False)
            kb.dma("pool", V[:], P.ap[:, C_AV + g * 128:C_AV + (g + 1) * 128].rearrange("(kc p) d -> p kc d", p=128),
                   V.sem, reads=[P.res], writes=[V.res], acc=False)
            for hh in range(4):
                h = g * 4 + hh
                qT = qTs[h % 2]
                kb.dma("sync", qT[:], QT.ap[h], qT.sem, reads=[QT.res], writes=[qT.res], acc=False)
                for qb in range(16):
                    self.attn_block(A, qT, qb * 128, kT, V, lat_chunks, False,
                                    OM.ap[qb * 128:(qb + 1) * 128, h * 128:(h + 1) * 128], OM.res)
                if keep_ctx:
                    for qb in (16, 17):
                        self.attn_block(A, qT, qb * 128, kT, V, [(2048, 256, None)], False,
                                        OM.ap[qb * 128:(qb + 1) * 128, h * 128:(h + 1) * 128], OM.res)
                self.attn_flush(A)
        self.stage_end(es)

    def stage_nat(self, l):
        kb = self.kb
        es = ExitStack()
        A = self.attn_ctx(es)
        keep_ctx = l < DEPTH - 1
        kTs = tiles(kb, es, "nkT", 2, [128, T], BF16, dma=True)
        Vs = tiles(kb, es, "nV", 2, [128, NTB, 128], BF16, dma=True)
        qTs = tiles(kb, es, "nqT", 2, [128, T], BF16, dma=True)
        nbs = tiles(kb, es, "nb", 2, [128, 5, 640], F32, dma=True)
        P, NQT, OM = self.P[l], self.NQT[l], self.OMIX[l]
        for h in range(4):
            kT, V, qT, nb = kTs[h % 2], Vs[h % 2], qTs[h % 2], nbs[h % 2]
            kb.dma("sync", kT[:], NQT.ap[4 + h], kT.sem, reads=[NQT.res], writes=[kT.res], acc=False)
            kb.dma("sync", qT[:], NQT.ap[h], qT.sem, reads=[NQT.res], writes=[qT.res], acc=False)
            kb.dma("sync", nb[:], self.c_nbias.ap[l, h].rearrange("s q k -> q s k"), nb.sem, writes=[nb.res], acc=False)
            kb.dma("pool", V[:], P.ap[:, C_NV + h * 128:C_NV + (h + 1) * 128].rearrange("(kc p) d -> p kc d", p=128),
                   V.sem, reads=[P.res], writes=[V.res], acc=False)
            col = 1536 + h * 128
            for qb in range(16):
                wb0 = min(max(qb - 2, 0), 11)
                pat = {0: 0, 1: 1, 14: 3, 15: 4}.get(qb, 2)
                chunks = [(wb0 * 128, 512, (nb[:, pat, 0:512], nb.res)),
                          ((wb0 + 4) * 128, 128, (nb[:, pat, 512:640], nb.res)),
                          (2048, 256, None)]
                self.attn_block(A, qT, qb * 128, kT, V, chunks, True,
                                OM.ap[qb * 128:(qb + 1) * 128, col:col + 128], OM.res)
            if keep_ctx:
                for qb in (16, 17):
                    self.attn_block(A, qT, qb * 128, kT, V, [(2048, 256, None)], True,
                                    OM.ap[qb * 128:(qb + 1) * 128, col:col + 128], OM.res)
            self.attn_flush(A)
        self.stage_end(es)

    def stage_hyshort(self, l):
        kb = self.kb
        es = ExitStack()
        P, U = self.P[l], self.U[l]
        wbc = Tile(kb, es, "wbc", [128, 4, 1536], F32, dma=True)
        for j in range(3):
            kb.dma("sync", wbc[:, j, :], self.hy_short_w.ap[l, j:j + 1, :].broadcast_to([128, 1536]), wbc.sem, writes=[wbc.res])
        kb.dma("sync", wbc[:, 3, :], self.hy_short_b.ap[l:l + 1, :].broadcast_to([128, 1536]), wbc.sem, writes=[wbc.res])
        hms = tiles(kb, es, "hm", 2, [128, 1536], F32, dma=True)
        h0s = tiles(kb, es, "h0", 2, [128, 1536], F32, dma=True)
        hps = tiles(kb, es, "hp", 2, [128, 1536], F32, dma=True)
        us = tiles(kb, es, "u", 2, [128, 1536], F32, dma=True)
        ta = Tile(kb, es, "hta", [128, 1536], F32)
        tb_ = Tile(kb, es, "htb", [128, 1536], F32)
        tc_ = Tile(kb, es, "htc", [128, 1536], F32)
        cs = slice(C_HY, C_HY + 1536)
        for tb in range(NTB):
            hm, h0, hp, u = hms[tb % 2], h0s[tb % 2], hps[tb % 2], us[tb % 2]
            r0 = tb * 128
            first = tb in (0, 16)
            last = tb in (15, 17)
            kb.dma("sync", h0[:], P.ap[r0:r0 + 128, cs], h0.sem, reads=[P.res], writes=[h0.res], acc=False)
            if first:
                kb.op("pool", lambda e: e.memset(hm[:], 0.0), writes=[hm.res])
                kb.dma("sync", hm[1:128, :], P.ap[r0:r0 + 127, cs], hm.sem, reads=[P.res, hm.res], writes=[hm.res], acc=True)
            else:
                kb.dma("sync", hm[:], P.ap[r0 - 1:r0 + 127, cs], hm.sem, reads=[P.res], writes=[hm.res], acc=False)
            if last:
                kb.op("pool", lambda e: e.memset(hp[:], 0.0), writes=[hp.res])
                kb.dma("sync", hp[0:127, :], P.ap[r0 + 1:r0 + 128, cs], hp.sem, reads=[P.res, hp.res], writes=[hp.res], acc=True)
            else:
                kb.dma("sync", hp[:], P.ap[r0 + 1:r0 + 129, cs], hp.sem, reads=[P.res], writes=[hp.res], acc=False)
            kb.op("dve", lambda e: e.tensor_tensor(out=ta[:], in0=hm[:], in1=wbc[:, 0, :], op=ALU.mult),
                  reads=[hm.res, wbc.res], writes=[ta.res])
            kb.op("pool", lambda e: e.tensor_tensor(out=tb_[:], in0=h0[:], in1=wbc[:, 1, :], op=ALU.mult),
                  reads=[h0.res, wbc.res], writes=[tb_.res])
            kb.op("pool", lambda e: e.tensor_tensor(out=tc_[:], in0=hp[:], in1=wbc[:, 2, :], op=ALU.mult),
                  reads=[hp.res, wbc.res], writes=[tc_.res])
            kb.op("dve", lambda e: e.tensor_tensor(out=ta[:], in0=ta[:], in1=tb_[:], op=ALU.add),
                  reads=[ta.res, tb_.res], writes=[ta.res])
            kb.op("pool", lambda e: e.tensor_tensor(out=tc_[:], in0=tc_[:], in1=wbc[:, 3, :], op=ALU.add),
                  reads=[tc_.res, wbc.res], writes=[tc_.res])
            kb.op("dve", lambda e: e.tensor_tensor(out=u[:], in0=ta[:], in1=tc_[:], op=ALU.add),
                  reads=[ta.res, tc_.res], writes=[u.res])
            kb.dma("sync", U.ap[r0:r0 + 128, :], u[:], u.sem, reads=[u.res], writes=[U.res])
        self.stage_end(es)

    def stage_hyfilter(self, l, ctx):
        kb = self.kb
        es = ExitStack()
        if ctx:
            n, NT, nf, dft, feat, win, HS, wfi = NCTX, 2, NFC, self.c_dftc, self.c_featc, self.c_winc, self.HSC[l], 1
        else:
            n, NT, nf, dft, feat, win, HS, wfi = NLAT, 16, NF, self.c_dft, self.c_feat, self.c_win, self.HS[l], 0
        w1t = Tile(kb, es, "w1t", [128, 128], F32, dma=True)
        w2t = Tile(kb, es, "w2t", [128, 128], F32, dma=True)
        kb.op("dve", lambda e: e.memset(w1t[:], 0.0), writes=[w1t.res])
        kb.op("dve", lambda e: e.memset(w2t[:], 0.0), writes=[w2t.res])
        w3t = Tile(kb, es, "w3t", [128, 2048], F32, dma=True)
        kb.op("dve", lambda e: e.memset(w3t[:], 0.0), writes=[w3t.res])
        cols = Tile(kb, es, "hcols", [128, 3], F32)
        crow = Tile(kb, es, "hcrow", [3, 128], F32, dma=True)
        kb.op("dve", lambda e: e.memset(crow[:], 0.0), writes=[crow.res])
        kb.dma("sync", w1t[0:33, 0:64], self.hf_w1.ap[l], w1t.sem, reads=[w1t.res], writes=[w1t.res])
        kb.dma("sync", w2t[0:64, 0:64], self.hf_w2.ap[l], w2t.sem, reads=[w2t.res], writes=[w2t.res])
        kb.dma("sync", w3t[0:64, :], self.hf_w3.ap[l], w3t.sem, reads=[w3t.res], writes=[w3t.res])
        for i, v in enumerate((self.hf_b1, self.hf_b2, self.hf_freq)):
            kb.dma("sync", crow[i:i + 1, 0:64], v.ap[l:l + 1, :], crow.sem, reads=[crow.res], writes=[crow.res])
        pbc, prc = kb.bank()
        kb.op("pe", lambda e: e.transpose(out=pbc[:, 0:3], in_=crow[:], identity=self.ident[0:3, 0:3]),
              reads=[crow.res, self.ident.res], writes=[prc])
        kb.op("dve", lambda e: e.tensor_copy(out=cols[:], in_=pbc[:, 0:3]), reads=[prc], writes=[cols.res])
        featT = Tile(kb, es, "featT", [128, n], F32, dma=True)
        kb.op("dve", lambda e: e.memset(featT[:], 0.0), writes=[featT.res])
        kb.dma("sync", featT[0:33, :], feat.ap, featT.sem, reads=[featT.res], writes=[featT.res])
        wf = Tile(kb, es, "wf", [128, NF], F32, dma=True)
        kb.dma("sync", wf[:], self.c_wf.ap[wfi], wf.sem, writes=[wf.res])
        wsp = Tile(kb, es, "wsp", [128, 2, 2, 128], BF16)
        w3s = Tile(kb, es, "w3s", [128, 2, 2048], BF16)
        fts = Tile(kb, es, "fts", [128, 2, n], BF16)
        h1s = Tile(kb, es, "h1s", [128, 2, n], BF16)
        h2s = Tile(kb, es, "h2s", [128, 2, n], BF16)
        self.split_bf16(w1t[:], w1t.res, wsp[:, 0, 0, :], wsp[:, 0, 1, :], wsp.res)
        self.split_bf16(w2t[:], w2t.res, wsp[:, 1, 0, :], wsp[:, 1, 1, :], wsp.res, acc=True)
        self.split_bf16(w3t[:], w3t.res, w3s[:, 0, :], w3s[:, 1, :], w3s.res)
        self.split_bf16(featT[:], featT.res, fts[:, 0, :], fts[:, 1, :], fts.res)
        hf = Tile(kb, es, "hf32", [128, 512], F32)
        arg = Tile(kb, es, "harg", [128, 512], F32)
        m1 = Tile(kb, es, "hm1", [128, 512], F32)
        PI = math.pi
        for (wi_, src, bcol, dst) in ((0, fts, 0, h1s), (1, h1s, 1, h2s)):
            for c0 in range(0, n, 512):
                cl = min(512, n - c0)
                pb, pr = kb.bank()
                self.mm3(pb[:, 0:cl], pr, wsp[:, wi_, 0, :], wsp[:, wi_, 1, :], src[:, 0, c0:c0 + cl], src[:, 1, c0:c0 + cl],
                         [wsp.res, src.res])
                a = arg[:, 0:cl]
                m = m1[:, 0:cl]
                kb.op("dve", lambda e: e.tensor_scalar(out=a, in0=pb[:, 0:cl], scalar1=cols[:, bcol:bcol + 1],
                                                       scalar2=cols[:, 2:3], op0=ALU.add, op1=ALU.mult),
                      reads=[pr, cols.res], writes=[arg.res])
                kb.op("dve", lambda e: e.tensor_scalar(out=m, in0=a, scalar1=PI, scalar2=-TWO_PI, op0=ALU.is_gt, op1=ALU.mult),
                      reads=[arg.res], writes=[m1.res])
                kb.op("dve", lambda e: e.tensor_tensor(out=a, in0=a, in1=m, op=ALU.add), reads=[arg.res, m1.res], writes=[arg.res])
                kb.op("dve", lambda e: e.tensor_scalar(out=m, in0=a, scalar1=-PI, scalar2=TWO_PI, op0=ALU.is_lt, op1=ALU.mult),
                      reads=[arg.res], writes=[m1.res])
                kb.op("dve", lambda e: e.tensor_tensor(out=a, in0=a, in1=m, op=ALU.add), reads=[arg.res, m1.res], writes=[arg.res])
                kb.op("dve", lambda e: e.tensor_scalar(out=a, in0=a, scalar1=PI, scalar2=-PI, op0=ALU.min, op1=ALU.max),
                      reads=[arg.res], writes=[arg.res])
                kb.op("act", lambda e: e.activation(out=hf[:, 0:cl], in_=a, func=AF.Sin), reads=[arg.res], writes=[hf.res])
                self.split_bf16(hf[:, 0:cl], hf.res, dst[:, 0, c0:c0 + cl], dst[:, 1, c0:c0 + cl], dst.res, acc=(c0 > 0))
        acc = Tile(kb, es, "hacc", [128, 2048], F32)
        fab = Tile(kb, es, "hfab", [128, 2048], F32)
        fws = tiles(kb, es, "hfw", 1, [128, 2048], F32)
        wts = tiles(kb, es, "hwin", 2, [128, 512], F32, dma=True)
        rn = Tile(kb, es, "hrn", [128, 2, 512], F32)
        tA = Tile(kb, es, "htA", [128, 2, 512], F32)
        tB = Tile(kb, es, "htB", [128, 2, 512], F32)
        A = Tile(kb, es, "hA", [128, NT, 2, 512], BF16)
        Bm = Tile(kb, es, "hBm", [128, NT, 2, 512], BF16)
        kb.op("dve", lambda e: e.memset(acc[:], 0.0), writes=[acc.res])
        it = 0
        for ps_ in range(2):
            for tc in range(NT):
                fw, wt_ = fws[0], wts[it % 2]
                it += 1
                kb.dma("sync", wt_[:], win.ap[tc * 128:(tc + 1) * 128, :], wt_.sem, writes=[wt_.res], acc=False)
                for q in range(4):
                    pb, pr = kb.bank()
                    self.mm3(pb, pr, h2s[:, 0, tc * 128:(tc + 1) * 128], h2s[:, 1, tc * 128:(tc + 1) * 128],
                             w3s[:, 0, q * 512:(q + 1) * 512], w3s[:, 1, q * 512:(q + 1) * 512], [h2s.res, w3s.res])
                    kb.op("dve", lambda e: e.tensor_tensor(out=fw[:, q * 512:(q + 1) * 512], in0=pb, in1=wt_[:], op=ALU.mult),
                          reads=[pr, wt_.res], writes=[fw.res], acc=(q > 0))
                fwv = fw[:].rearrange("p (o d c) -> p o d c", o=2, d=2)
                if tc == 0:
                    kb.op("dve", lambda e: e.memset(fwv[0:1, :, 1, :], 0.0), reads=[fw.res], writes=[fw.res])
                if ps_ == 0:
                    kb.op("dve", lambda e: e.scalar_tensor_tensor(out=fab[:], in0=fw[:], scalar=-1.0, in1=fw[:],
                                                                  op0=ALU.mult, op1=ALU.max),
                          reads=[fw.res], writes=[fab.res])
                    kb.op("dve", lambda e: e.tensor_tensor(out=acc[:], in0=acc[:], in1=fab[:], op=ALU.add),
                          reads=[fab.res, acc.res], writes=[acc.res])
                else:
                    kb.op("dve", lambda e: e.tensor_tensor(out=tA[:], in0=fwv[:, :, 0, :], in1=fwv[:, :, 1, :], op=ALU.add),
                          reads=[fw.res], writes=[tA.res])
                    kb.op("pool", lambda e: e.tensor_tensor(out=tB[:], in0=fwv[:, :, 0, :], in1=fwv[:, :, 1, :], op=ALU.subtract),
                          reads=[fw.res], writes=[tB.res])
                    kb.op("dve", lambda e: e.tensor_tensor(out=A[:, tc, :, :], in0=tA[:], in1=rn[:], op=ALU.mult),
                          reads=[tA.res, rn.res], writes=[A.res], acc=(tc > 0))
                    kb.op("pool", lambda e: e.tensor_tensor(out=Bm[:, tc, :, :], in0=tB[:], in1=rn[:], op=ALU.mult),
                          reads=[tB.res, rn.res], writes=[Bm.res], acc=(tc > 0))
            if ps_ == 0:
                bks = []
                self.split_bf16(acc[:], acc.res, w3s[:, 0, :], w3s[:, 1, :], w3s.res)
                for q in range(4):
                    pb, pr = kb.bank()
                    kb.op("pe", lambda e: e.matmul(pb, lhsT=self.ones[:], rhs=w3s[:, 0, q * 512:(q + 1) * 512], start=True, stop=False),
                          reads=[self.ones.res, w3s.res], writes=[pr], mark=False)
                    kb.op("pe", lambda e: e.matmul(pb, lhsT=self.ones[:], rhs=w3s[:, 1, q * 512:(q + 1) * 512], start=False, stop=True),
                          reads=[self.ones.res, w3s.res], writes=[pr])
                    bks.append((pb, pr))
                self.split_bf16(w3t[:], w3t.res, w3s[:, 0, :], w3s[:, 1, :], w3s.res)
                for o in range(2):
                    kb.op("act", lambda e: e.copy(out=tA[:, o, :], in_=bks[2 * o + 1][0]), reads=[bks[2 * o + 1][1]],
                          writes=[tA.res], acc=(o > 0))
                    kb.op("dve", lambda e: e.scalar_tensor_tensor(out=tB[:, o, :], in0=bks[2 * o][0], scalar=EPS, in1=tA[:, o, :],
                                                                  op0=ALU.add, op1=ALU.add),
                          reads=[bks[2 * o][1], tA.res], writes=[tB.res], acc=(o > 0))
                kb.op("dve", lambda e: e.reciprocal(out=rn[:], in_=tB[:]), reads=[tB.res], writes=[rn.res])
        Cps = tiles(kb, es, "hCp", 2, [128, nf, 128], BF16, dma=True)
        Sps = tiles(kb, es, "hSp", 2, [128, nf, 128], BF16, dma=True)
        hsts = tiles(kb, es, "hst", 1, [128, 4, 512], F32, dma=True)
        for j in range(nf):
            Cp, Sp, hst = Cps[j % 2], Sps[j % 2], hsts[0]
            kb.dma("sync", Cp[:], dft.ap[0, j], Cp.sem, writes=[Cp.res], acc=False)
            kb.dma("sync", Sp[:], dft.ap[1, j], Sp.sem, writes=[Sp.res], acc=False)
            for o in range(2):
                for ri, (pan, src) in enumerate(((Cp, A), (Sp, Bm))):
                    pb, pr = kb.bank()
                    for tc in range(NT):
                        kb.op("pe", lambda e: e.matmul(pb, lhsT=pan[:, tc, :], rhs=src[:, tc, o, :], start=(tc == 0), stop=(tc == NT - 1)),
                              reads=[pan.res, src.res], writes=[pr], mark=(tc == NT - 1))
                    first = (o == 0 and ri == 0)
                    kb.op("act", lambda e: e.activation(out=hst[:, o * 2 + ri, :], in_=pb, func=AF.Identity, scale=wf[:, j:j + 1]),
                          reads=[pr, wf.res], writes=[hst.res], acc=(not first))
            kb.dma("sync", HS.ap[j * 128:(j + 1) * 128], hst[:], hst.sem, reads=[hst.res], writes=[HS.res])
        self.stage_end(es)

    def stage_hyconv(self, l, ctx):
        kb = self.kb
        es = ExitStack()
        if ctx:
            n, NT, nf, dft, HS, r0 = NCTX, 2, NFC, self.c_dftc, self.HSC[l], NLAT
        else:
            n, NT, nf, dft, HS, r0 = NLAT, 16, NF, self.c_dft, self.HS[l], 0
        U, Z1, OM = self.U[l], self.Z1[l], self.OMIX[l]
        zbs = tiles(kb, es, "zb", 2, [128, NT, 512], BF16, dma=True)
        Y = Tile(kb, es, "hY", [128, nf, 2, 512], BF16)
        Cps = tiles(kb, es, "cCp", 2, [128, nf, 128], BF16, dma=True)
        Sps = tiles(kb, es, "cSp", 2, [128, nf, 128], BF16, dma=True)
        Hts = tiles(kb, es, "cHt", 2, [128, 2, 512], F32, dma=True)
        t1 = Tile(kb, es, "ct1", [128, 512], F32)
        t2 = Tile(kb, es, "ct2", [128, 512], F32)
        t3 = Tile(kb, es, "ct3", [128, 512], F32)
        t4 = Tile(kb, es, "ct4", [128, 512], F32)
        zts = tiles(kb, es, "czt", 2, [128, 512], F32, dma=True)
        gts = tiles(kb, es, "cgt", 2, [128, 512], F32, dma=True)
        osts = tiles(kb, es, "cost", 2, [128, 512], F32, dma=True)
        skip = Tile(kb, es, "cskip", [128, 2, 512], F32, dma=True)
        for o in range(2):
            kb.dma("sync", skip[:, o, :], self.hy_bias.ap[l, o:o + 1, :].broadcast_to([128, 512]), skip.sem, writes=[skip.res])
        pi = 0
        for o in range(2):
            zsrc_ap = U.ap[r0:r0 + n, 0:512] if o == 0 else Z1.ap[r0:r0 + n, :]
            zsrc_res = U.res if o == 0 else Z1.res
            zb = zbs[o]
            kb.dma("pool", zb[:], zsrc_ap.rearrange("(tc p) c -> p tc c", p=128), zb.sem, reads=[zsrc_res], writes=[zb.res], acc=False)
            for j in range(nf):
                Cp, Sp, Ht = Cps[pi % 2], Sps[pi % 2], Hts[pi % 2]
                pi += 1
                kb.dma("sync", Cp[:], dft.ap[0, j], Cp.sem, writes=[Cp.res], acc=False)
                kb.dma("sync", Sp[:], dft.ap[1, j], Sp.sem, writes=[Sp.res], acc=False)
                kb.dma("sync", Ht[:], HS.ap[j * 128:(j + 1) * 128, 2 * o:2 * o + 2, :], Ht.sem, reads=[HS.res], writes=[Ht.res], acc=False)
                pre, prr = kb.bank()
                for tc in range(NT):
                    kb.op("pe", lambda e: e.matmul(pre, lhsT=Cp[:, tc, :], rhs=zb[:, tc, :], start=(tc == 0), stop=(tc == NT - 1)),
                          reads=[Cp.res, zb.res], writes=[prr], mark=(tc == NT - 1))
                pim, pir = kb.bank()
                for tc in range(NT):
                    kb.op("pe", lambda e: e.matmul(pim, lhsT=Sp[:, tc, :], rhs=zb[:, tc, :], start=(tc == 0), stop=(tc == NT - 1)),
                          reads=[Sp.res, zb.res], writes=[pir], mark=(tc == NT - 1))
                kb.op("dve", lambda e: e.tensor_tensor(out=t1[:], in0=pre, in1=Ht[:, 0, :], op=ALU.mult), reads=[prr, Ht.res], writes=[t1.res])
                kb.op("dve", lambda e: e.tensor_tensor(out=t2[:], in0=pim, in1=Ht[:, 1, :], op=ALU.mult), reads=[pir, Ht.res], writes=[t2.res])
                kb.op("pool", lambda e: e.tensor_tensor(out=Y[:, j, 0, :], in0=t1[:], in1=t2[:], op=ALU.subtract),
                      reads=[t1.res, t2.res], writes=[Y.res], acc=(j > 0))
                kb.op("dve", lambda e: e.tensor_tensor(out=t3[:], in0=pim, in1=Ht[:, 0, :], op=ALU.mult), reads=[pir, Ht.res], writes=[t3.res])
                kb.op("dve", lambda e: e.tensor_tensor(out=t4[:], in0=pre, in1=Ht[:, 1, :], op=ALU.mult), reads=[prr, Ht.res], writes=[t4.res])
                kb.op("pool", lambda e: e.tensor_tensor(out=Y[:, j, 1, :], in0=t3[:], in1=t4[:], op=ALU.add),
                      reads=[t3.res, t4.res], writes=[Y.res], acc=True)
            for i in range(NT):
                Cp, Sp = Cps[pi % 2], Sps[pi % 2]
                zt, gt, ost = zts[i % 2], gts[i % 2], osts[i % 2]
                pi += 1
                rows = slice(r0 + i * 128, r0 + (i + 1) * 128)
                kb.dma("sync", Cp[:], dft.ap[0, i], Cp.sem, writes=[Cp.res], acc=False)
                kb.dma("sync", Sp[:], dft.ap[1, i], Sp.sem, writes=[Sp.res], acc=False)
                if o == 0:
                    kb.dma("sync", zt[:], U.ap[rows, 0:512], zt.sem, reads=[U.res], writes=[zt.res], acc=False)
                else:
                    kb.dma("sync", zt[:], Z1.ap[rows, :], zt.sem, reads=[Z1.res], writes=[zt.res], acc=False)
                kb.dma("sync", gt[:], U.ap[rows, 512 * (o + 1):512 * (o + 2)], gt.sem, reads=[U.res], writes=[gt.res], acc=False)
                py, pyr = kb.bank()
                k = 0
                for fc in range(nf):
                    for ri, pan in enumerate((Cp, Sp)):
                        kb.op("pe", lambda e: e.matmul(py, lhsT=pan[:, fc, :], rhs=Y[:, fc, ri, :], start=(k == 0), stop=(k == 2 * nf - 1)),
                              reads=[pan.res, Y.res], writes=[pyr], mark=(k == 2 * nf - 1))
                        k += 1
                kb.op("dve", lambda e: e.tensor_tensor(out=t1[:], in0=zt[:], in1=skip[:, o, :], op=ALU.mult),
                      reads=[zt.res, skip.res], writes=[t1.res])
                kb.op("dve", lambda e: e.tensor_tensor(out=t2[:], in0=py, in1=t1[:], op=ALU.add), reads=[pyr, t1.res], writes=[t2.res])
                kb.op("pool", lambda e: e.tensor_tensor(out=ost[:], in0=t2[:], in1=gt[:], op=ALU.mult),
                      reads=[t2.res, gt.res], writes=[ost.res])
                if o == 0:
                    kb.dma("sync", Z1.ap[rows, :], ost[:], ost.sem, reads=[ost.res], writes=[Z1.res])
                else:
                    kb.dma("sync", OM.ap[rows, 1024:1536], ost[:], ost.sem, reads=[ost.res], writes=[OM.res])
        self.stage_end(es)

    def layernorm(self, LN, r, y, g, b):
        kb = self.kb
        st, mv, sd, epsb = LN["st"], LN["mv"], LN["sd"], LN["epsb"]
        for c in range(4):
            kb.op("dve", lambda e: e.bn_stats(out=st[:, c, :], in_=r[:, c * 512:(c + 1) * 512]), reads=[r.res], writes=[st.res], acc=(c > 0))
        kb.op("dve", lambda e: e.bn_aggr(out=mv[:], in_=st[:].rearrange("p c s -> p (c s)")), reads=[st.res], writes=[mv.res])
        kb.op("act", lambda e: e.activation(out=sd[:], in_=mv[:, 1:2], func=AF.Sqrt, bias=epsb[:]), reads=[mv.res, epsb.res], writes=[sd.res])
        kb.op("dve", lambda e: e.reciprocal(out=sd[:], in_=sd[:]), reads=[sd.res], writes=[sd.res])
        kb.op("dve", lambda e: e.tensor_scalar(out=r[:], in0=r[:], scalar1=mv[:, 0:1], scalar2=sd[:], op0=ALU.subtract, op1=ALU.mult),
              reads=[r.res, mv.res, sd.res], writes=[r.res])
        kb.op("pool", lambda e: e.tensor_tensor(out=y[:], in0=r[:], in1=g[:], op=ALU.mult), reads=[r.res, g.res], writes=[y.res])
        kb.op("pool", lambda e: e.tensor_tensor(out=y[:], in0=y[:], in1=b[:], op=ALU.add), reads=[y.res, b.res], writes=[y.res])

    def ln_ctx(self, es):
        kb = self.kb
        LN = {"st": Tile(kb, es, "ln_st", [128, 4, 6], F32), "mv": Tile(kb, es, "ln_mv", [128, 2], F32),
              "sd": Tile(kb, es, "ln_sd", [128, 1], F32), "epsb": Tile(kb, es, "ln_eps", [128, 1], F32)}
        kb.op("dve", lambda e: e.memset(LN["epsb"][:], EPS), writes=[LN["epsb"].res])
        return LN

    def load_bc(self, es, name, src_ap, width, src_res=None):
        t = Tile(self.kb, es, name, [128, width], F32, dma=True)
        self.kb.dma("sync", t[:], src_ap.broadcast_to([128, width]), t.sem,
                    reads=([src_res] if src_res is not None else []), writes=[t.res])
        return t

    def stage_outproj(self, l, xsrc):
        kb = self.kb
        es = ExitStack()
        keep_ctx = l < DEPTH - 1
        nblk = NTB if keep_ctx else 16
        OM, X1 = self.OMIX[l], self.X1[l]
        LN = self.ln_ctx(es)
        Wo = Tile(kb, es, "Wo", [128, 16, D], BF16, dma=True)
        for j in range(4):
            kb.dma("pool", Wo[:, :, j * 512:(j + 1) * 512],
                   self.w_out.ap[l, :, j * 512:(j + 1) * 512].rearrange("(kc p) n -> p kc n", p=128), Wo.sem, writes=[Wo.res])
        gmf = Tile(kb, es, "gmf", [128, 16], F32)
        self.load_fm(es, "gmfl", self.g_mix.ap[l], 16, gmf[:], gmf.res)
        gts = [self.load_bc(es, "gt1l", self.MODd[l].ap[0:1, 2 * D:3 * D], D, self.MODd[l].res)]
        if keep_ctx:
            gts.append(self.load_bc(es, "gt1c", self.MODd[l].ap[1:2, 2 * D:3 * D], D, self.MODd[l].res))
        lng = self.load_bc(es, "lng", self.ln1_g.ap[l:l + 1, :], D)
        lnb = self.load_bc(es, "lnb", self.ln1_b.ap[l:l + 1, :], D)
        oms = tiles(kb, es, "om", 2, [128, D], F32, dma=True)
        xos = tiles(kb, es, "xo", 2, [128, D], F32, dma=True)
        rs = tiles(kb, es, "rr", 2, [128, D], F32, dma=True)
        UTs = tiles(kb, es, "UTb", 2, [128, 16, 128], BF16)
        sqj = Tile(kb, es, "sqj", [128, 1024], F32)
        ss3 = Tile(kb, es, "ss3", [128, 3], F32)
        sd3 = Tile(kb, es, "sd3", [128, 3], F32)
        groups = ((0, 1024), (1024, 512), (1536, 512))
        for tb in range(nblk):
            om, xo, r, UTb = oms[tb % 2], xos[tb % 2], rs[tb % 2], UTs[tb % 2]
            y = r
            rows = slice(tb * 128, (tb + 1) * 128)
            kb.dma("sync", om[:], OM.ap[rows, :], om.sem, reads=[OM.res], writes=[om.res], acc=False)
            kb.dma("sync", xo[:], xsrc.ap[rows, :], xo.sem, reads=[xsrc.res], writes=[xo.res], acc=False)
            for gi, (c0, w) in enumerate(groups):
                kb.op("act", lambda e: e.activation(out=sqj[:, 0:w], in_=om[:, c0:c0 + w], func=AF.Square, accum_out=ss3[:, gi:gi + 1]),
                      reads=[om.res], writes=[sqj.res, ss3.res], acc=False)
                kb.op("act", lambda e: e.activation(out=sd3[:, gi:gi + 1], in_=ss3[:, gi:gi + 1], func=AF.Sqrt, scale=1.0 / w, bias=LN["epsb"][:]),
                      reads=[ss3.res, LN["epsb"].res], writes=[sd3.res], acc=(gi > 0))
            kb.op("dve", lambda e: e.reciprocal(out=sd3[:], in_=sd3[:]), reads=[sd3.res], writes=[sd3.res])
            for gi, (c0, w) in enumerate(groups):
                kb.op("dve", lambda e: e.tensor_scalar(out=om[:, c0:c0 + w], in0=om[:, c0:c0 + w], scalar1=sd3[:, gi:gi + 1], scalar2=None, op0=ALU.mult),
                      reads=[om.res, sd3.res], writes=[om.res])
            self.build_ut_block(om, UTb, UTb.res, 0, lambda kc: gmf[:, kc:kc + 1], lambda kc: 0.0, [gmf.res])
            gt = gts[0 if tb < 16 else 1]
            for j in range(4):
                pb, pr = kb.bank()
                for kc in range(16):
                    kb.op("pe", lambda e: e.matmul(pb, lhsT=UTb[:, kc, :], rhs=Wo[:, kc, j * 512:(j + 1) * 512], start=(kc == 0), stop=(kc == 15)),
                          reads=[UTb.res, Wo.res], writes=[pr], mark=(kc == 15))
                kb.op("dve", lambda e: e.tensor_tensor(out=r[:, j * 512:(j + 1) * 512], in0=pb, in1=gt[:, j * 512:(j + 1) * 512], op=ALU.mult),
                      reads=[pr, gt.res], writes=[r.res], acc=(j > 0))
            kb.op("dve", lambda e: e.scalar_tensor_tensor(out=r[:], in0=xo[:], scalar=DN_ALPHA, in1=r[:], op0=ALU.mult, op1=ALU.add),
                  reads=[xo.res, r.res], writes=[r.res])
            self.layernorm(LN, r, y, lng, lnb)
            kb.dma("sync", X1.ap[rows, :], y[:], y.sem, reads=[y.res], writes=[X1.res])
        self.stage_end(es)

    def stage_ffn_up(self, l):
        kb = self.kb
        es = ExitStack()
        keep_ctx = l < DEPTH - 1
        nblk = NTB if keep_ctx else 16
        ntok = nblk * 128
        X1, HT = self.X1[l], self.HT[l]
        MF, MF1 = self.load_modf(es, l)
        UT = Tile(kb, es, "UT2", [128, 16, T], BF16)
        xts = tiles(kb, es, "xt2", 2, [128, D], F32, dma=True)
        utres = [Res() for _ in range(NTB)]
        for tb in range(nblk):
            xt = xts[tb % 2]
            kb.dma("sync", xt[:], X1.ap[tb * 128:(tb + 1) * 128, :], xt.sem, reads=[X1.res], writes=[xt.res], acc=False)
            r = 0 if tb < 16 else 1
            self.build_ut_block(xt, UT, utres[tb], tb * 128,
                                lambda kc: MF1[:, r, 64 + kc:65 + kc], lambda kc: MF[:, r, 48 + kc:49 + kc],
                                [MF.res, MF1.res])
        wp = tiles(kb, es, "w1p", 3, [128, 16, 512], BF16, dma=True)
        r32 = tiles(kb, es, "r32", 3, [128, 512], F32)
        hst = tiles(kb, es, "hst", 4, [128, 512], BF16, dma=True)
        it = 0
        for jp in range(DFF // 512):
            w = wp[jp % 3]
            kb.dma("pool", w[:], self.w1.ap[l, :, jp * 512:(jp + 1) * 512].rearrange("(kc p) n -> p kc n", p=128),
                   w.sem, writes=[w.res], acc=False)
            for fc in range(4):
                for t0 in range(0, ntok, 512):
                    tl = min(512, ntok - t0)
                    ur = [utres[b] for b in range(t0 // 128, (t0 + tl) // 128)]
                    pb, pr = kb.bank()
                    for kc in range(16):
                        kb.op("pe", lambda e: e.matmul(pb[:, 0:tl], lhsT=w[:, kc, fc * 128:(fc + 1) * 128], rhs=UT[:, kc, t0:t0 + tl],
                                                       start=(kc == 0), stop=(kc == 15)),
                              reads=ur + [w.res], writes=[pr], mark=(kc == 15))
                    rr, hs = r32[it % 3], hst[it % 4]
                    it += 1
                    kb.op("act", lambda e: e.activation(out=rr[:, 0:tl], in_=pb[:, 0:tl], func=AF.Relu), reads=[pr], writes=[rr.res])
                    kb.op("pool", lambda e: e.tensor_tensor(out=hs[:, 0:tl], in0=rr[:, 0:tl], in1=rr[:, 0:tl], op=ALU.mult),
                          reads=[rr.res], writes=[hs.res])
                    f0 = jp * 512 + fc * 128
                    kb.dma("sync", HT.ap[f0:f0 + 128, t0:t0 + tl], hs[:, 0:tl], hs.sem, reads=[hs.res], writes=[HT.res])
        self.stage_end(es)

    def stage_ffn_down(self, l):
        kb = self.kb
        es = ExitStack()
        keep_ctx = l < DEPTH - 1
        last = l == DEPTH - 1
        nblk = NTB if keep_ctx else 16
        ntok = nblk * 128
        X1, X2, HT = self.X1[l], self.X2[l], self.HT[l]
        LN = self.ln_ctx(es)
        lng = self.load_bc(es, "lng2", self.ln2_g.ap[l:l + 1, :], D)
        lnb = self.load_bc(es, "lnb2", self.ln2_b.ap[l:l + 1, :], D)
        gt = Tile(kb, es, "gt2", [128, D], F32, dma=True)
        HTg = Tile(kb, es, "HTg", [128, 64, 512], BF16, dma=True)
        wp = tiles(kb, es, "w2p", 3, [128, 16, 512], BF16, dma=True)
        htres = [Res() for _ in range(4)]
        rs = tiles(kb, es, "r2", 4, [128, D], F32, dma=True)
        x1t = Tile(kb, es, "x1t", [128, D], F32, dma=True)
        wi = 0
        yi = 0
        cur_gt = None
        for t0 in range(0, ntok, 512):
            tl = min(512, ntok - t0)
            nm = tl // 128
            row = 0 if t0 < NLAT else 1
            if cur_gt != row:
                kb.dma("sync", gt[:], self.MODd[l].ap[row:row + 1, 5 * D:6 * D].broadcast_to([128, D]), gt.sem,
                       reads=[self.MODd[l].res], writes=[gt.res], acc=False)
                cur_gt = row
            for kq in range(4):
                kb.dma("sync", HTg[:, kq * 16:(kq + 1) * 16, 0:tl],
                       HT.ap[kq * 2048:(kq + 1) * 2048, t0:t0 + tl].rearrange("(kc p) t -> p kc t", p=128), HTg.sem,
                       reads=[HT.res], writes=[htres[kq]], acc=False)
            for j in range(4):
                bks = [kb.bank() for _ in range(nm)]
                for kq in range(4):
                    w = wp[wi % 3]
                    wi += 1
                    kb.dma("pool", w[:], self.w2.ap[l, kq * 2048:(kq + 1) * 2048, j * 512:(j + 1) * 512].rearrange("(kc p) n -> p kc n", p=128),
                           w.sem, writes=[w.res], acc=False)
                    for m in range(nm):
                        pb, pr = bks[m]
                        for kc in range(16):
                            kb.op("pe", lambda e: e.matmul(pb, lhsT=HTg[:, kq * 16 + kc, m * 128:(m + 1) * 128], rhs=w[:, kc, :],
                                                           start=(kq == 0 and kc == 0), stop=(kq == 3 and kc == 15)),
                                  reads=[htres[kq], w.res], writes=[pr], mark=(kc == 15))
                for m in range(nm):
                    pb, pr = bks[m]
                    kb.op("dve", lambda e: e.tensor_tensor(out=rs[m][:, j * 512:(j + 1) * 512], in0=pb, in1=gt[:, j * 512:(j + 1) * 512], op=ALU.mult),
                          reads=[pr, gt.res], writes=[rs[m].res], acc=(j > 0))
            for m in range(nm):
                r = rs[m]
                y = r
                rows = slice(t0 + m * 128, t0 + (m + 1) * 128)
                kb.dma("sync", x1t[:], X1.ap[rows, :], x1t.sem, reads=[X1.res], writes=[x1t.res], acc=False)
                kb.op("dve", lambda e: e.scalar_tensor_tensor(out=r[:], in0=x1t[:], scalar=DN_ALPHA, in1=r[:], op0=ALU.mult, op1=ALU.add),
                      reads=[x1t.res, r.res], writes=[r.res])
                self.layernorm(LN, r, y, lng, lnb)
                if last:
                    kb.dma("sync", self.yout.ap[rows, :], y[:], y.sem, reads=[y.res], writes=[self.yout.res])
                else:
                    kb.dma("sync", X2.ap[rows, :], y[:], y.sem, reads=[y.res], writes=[X2.res])
        self.stage_end(es)

    def layer_stages(self, l):
        xsrc = self.xin if l == 0 else self.X2[l - 1]
        keep_ctx = l < DEPTH - 1
        self.stage_inproj(l, xsrc)
        self.stage_qkprep(l)
        self.stage_gqa(l)
        self.stage_nat(l)
        self.stage_hyshort(l)
        self.stage_hyfilter(l, False)
        self.stage_hyconv(l, False)
        if keep_ctx:
            self.stage_hyfilter(l, True)
            self.stage_hyconv(l, True)
        self.stage_outproj(l, xsrc)
        self.stage_ffn_up(l)
        self.stage_ffn_down(l)

    def build(self, stages):
        self.declare()
        self.setup_globals()
        for s in stages:
            s(self)
        self.kb.barrier()
        self.glob.close()
        return self.nc


def _bf16(a):
    import ml_dtypes
    return np.asarray(a, dtype=np.float32).astype(ml_dtypes.bfloat16)


def _dft_blocked(n_chunks, L):
    n = n_chunks * 128
    idx = np.arange(n, dtype=np.int64)
    prod = (idx[:, None] * idx[None, :]) % L
    ang = prod.astype(np.float64) * (2.0 * np.pi / L)
    out = np.empty((2, n_chunks, 128, n_chunks, 128), dtype=np.float32)
    for k, tab in enumerate((np.cos(ang), np.sin(ang))):
        out[k] = tab.reshape(n_chunks, 128, n_chunks, 128).transpose(2, 1, 0, 3)
    return _bf16(out)


def _hy_feat(n):
    t = np.linspace(0.0, 1.0, n, dtype=np.float32)
    bands = np.arange(1, 17, dtype=np.float32)
    ang = (2.0 * np.float32(math.pi) * t[:, None] * bands[None, :]).astype(np.float32)
    feat = np.concatenate([t[:, None], np.cos(ang), np.sin(ang)], -1).astype(np.float32)
    max_decay = math.log(1e-2) / 0.3
    min_decay = math.log(1e-2) / 1.5
    deltas = np.abs(np.linspace(min_decay, max_decay, 512, dtype=np.float32))
    win = (np.exp(-t[:, None] * deltas[None, :]) + np.float32(0.05)).astype(np.float32)
    return np.ascontiguousarray(feat.T), win


def _nat_index():
    rows = NLAT // GRID_W
    pats = {0: 0, 1: 1, 2: 5, 3: 14, 4: 15}
    valid = np.zeros((5, 128, 640), dtype=bool)
    drow = np.zeros((5, 128, 640), dtype=np.int64)
    dcol = np.zeros((5, 128, 640), dtype=np.int64)
    for pi, j in pats.items():
        wb0 = min(max(j - 2, 0), 11)
        for q in range(128):
            tq = j * 128 + q
            r, c = tq // GRID_W, tq % GRID_W
            r0 = min(max(r - 4, 0), rows - 8)
            c0 = min(max(c - 8, 0), GRID_W - 16)
            for k in range(640):
                tk = wb0 * 128 + k
                kr, kc = tk // GRID_W, tk % GRID_W
                if r0 <= kr < r0 + 8 and c0 <= kc < c0 + 16:
                    valid[pi, q, k] = True
                    drow[pi, q, k] = kr - r + 7
                    dcol[pi, q, k] = min(max(kc - c + 15, 0), 30)
    return valid, drow, dcol


_CONST_CACHE = {}


def make_consts(nat_rpb):
    if "c" not in _CONST_CACHE:
        c = {}
        c["c_ident"] = np.eye(128, dtype=np.float32)
        t = np.arange(NLAT)
        row = (t // GRID_W).astype(np.float32)
        col = (t % GRID_W).astype(np.float32)
        inv = (np.float32(10000.0) ** (-np.arange(32, dtype=np.float32) / np.float32(32))).astype(np.float32)
        ang = np.concatenate([row[:, None] * inv[None, :], col[:, None] * inv[None, :]], -1).astype(np.float32)
        c["c_rope"] = np.stack([np.cos(ang), np.sin(ang)]).astype(np.float32)
        c["c_dft"] = _dft_blocked(NF, 4096)
        c["c_dftc"] = _dft_blocked(NFC, 512)
        wf = np.zeros((2, NF * 128), dtype=np.float32)
        wf[0, :2049] = 2.0 / 4096
        wf[0, 0] = wf[0, 2048] = 1.0 / 4096
        wf[1, :257] = 2.0 / 512
        wf[1, 0] = wf[1, 256] = 1.0 / 512
        c["c_wf"] = np.ascontiguousarray(wf.reshape(2, NF, 128).transpose(0, 2, 1))
        c["c_feat"], c["c_win"] = _hy_feat(NLAT)
        c["c_featc"], c["c_winc"] = _hy_feat(NCTX)
        c["_nat"] = _nat_index()
        _CONST_CACHE["c"] = c
    c = dict(_CONST_CACHE["c"])
    valid, drow, dcol = c.pop("_nat")
    rpb = np.asarray(nat_rpb, dtype=np.float32)
    g = rpb[:, :, drow, dcol]
    c["c_nbias"] = np.where(valid[None, None], g, np.float32(NEG)).astype(np.float32)
    return c


def core_inputs(inputs, b, consts):
    m = {}
    m["xin"] = np.ascontiguousarray(np.concatenate([inputs["x"][b], inputs["ctx"][b]], 0), dtype=np.float32)
    m["cvec"] = np.ascontiguousarray(np.stack([inputs["c"][b], inputs["c_ctx"]]), dtype=np.float32)
    for k in ("w_mod", "b_mod", "w_in", "q_norm_g", "k_norm_g", "hy_short_w", "hy_short_b", "hf_w1", "hf_b1",
              "hf_freq", "hf_w2", "hf_b2", "hf_w3", "hy_bias", "g_mix", "w_out", "ln1_g", "ln1_b", "w1", "w2",
              "ln2_g", "ln2_b"):
        m[k] = np.ascontiguousarray(inputs[k], dtype=np.float32)
    m.update(consts)
    return m


_PROG_CACHE = {}


def _get_prog():
    if "p" not in _PROG_CACHE:
        p = Prog(dbg=None)
        p.build([lambda q: q.stage_mod()] + [(lambda q, l=l: q.layer_stages(l)) for l in range(DEPTH)])
        _PROG_CACHE["p"] = p
    return _PROG_CACHE["p"]


def kernel(**inputs):
    inputs = {k: np.asarray(v) for k, v in inputs.items()}
    consts = make_consts(inputs["nat_rpb"])
    p = _get_prog()
    n = 8
    in_maps = []
    for b in range(n):
        m = core_inputs(inputs, b, consts)
        in_maps.append({k: v for k, v in m.items() if k in p.inputs})
    res = run_bass_kernel_spmd(p.nc, in_maps, core_ids=list(range(n)))
    out = np.stack([np.asarray(r["yout"], dtype=np.float32) for r in res.results], axis=0)
    return out
```
